# Optimizing a Trainium2 kernel written in Bass

```python
import jax, jax.numpy as jnp
from jax import lax
import numpy as np

D_MODEL = 2048
BATCH = 8
SEQ = 2048
DEPTH = 2

MEM_LEN = 256
RMS_EPS = 1e-6
CHUNK = 64
HGRN_WIDTH = D_MODEL // 2
HGRN_HEAD_K = 128
HGRN_HEADS = HGRN_WIDTH // HGRN_HEAD_K
HGRN_HEAD_V = HGRN_WIDTH // HGRN_HEADS
GLA_WIDTH = D_MODEL - HGRN_WIDTH
GLA_HEADS = 4
GLA_HEAD_V = GLA_WIDTH // GLA_HEADS
GLA_HEAD_K = GLA_HEAD_V // 2
GLA_GATE_RANK = 16
GLA_GATE_NORMALIZER = 16.0
AB_SPLITS = (
    HGRN_HEADS * HGRN_HEAD_K,
    HGRN_HEADS * HGRN_HEAD_K,
    HGRN_HEADS * HGRN_HEAD_V,
    HGRN_HEADS * HGRN_HEAD_V,
    GLA_HEADS * GLA_HEAD_K,
    GLA_HEADS * GLA_HEAD_K,
    GLA_HEADS * GLA_HEAD_V,
    GLA_HEADS * GLA_HEAD_V,
    GLA_GATE_RANK,
)
AB_IN = sum(AB_SPLITS)
SB_HEADS = 16
SB_HEAD_DIM = D_MODEL // SB_HEADS
SB_QBLOCK = 128
XA_HEADS = 4
XA_HEAD_DIM = D_MODEL // XA_HEADS
D_FF = 5632
CONV_W = 3
N_AB = (DEPTH + 1) // 2
N_C = DEPTH // 2

kernel_name = "hybrid_hgrn2_gla_stickbreaking_trunk"


def rms_norm(x, gain):
    x32 = x.astype(jnp.float32)
    y = x32 * lax.rsqrt(jnp.mean(x32 * x32, axis=-1, keepdims=True) + RMS_EPS)
    return (y * gain.astype(jnp.float32)).astype(x.dtype)


def chunked_gated_linear_attention(q, k, v, log_g):
    out_dtype = v.dtype
    b_, s_, h_, k_ = q.shape
    v_ = v.shape[-1]
    n_ = s_ // CHUNK

    def to_chunks(a):
        return a.astype(jnp.float32).reshape(b_, n_, CHUNK, h_, a.shape[-1]).transpose(0, 3, 1, 2, 4)

    qc, kc, vc, gc = to_chunks(q), to_chunks(k), to_chunks(v), to_chunks(log_g)
    b = jnp.cumsum(gc, axis=3)
    b_mid = b[:, :, :, CHUNK // 2 - 1:CHUNK // 2, :]
    b_last = b[:, :, :, CHUNK - 1:, :]
    causal = jnp.tril(jnp.ones((CHUNK, CHUNK), dtype=bool))
    scores = jnp.einsum('bhnck,bhndk->bhncd', qc * jnp.exp(b - b_mid), kc * jnp.exp(b_mid - b))
    scores = jnp.where(causal, scores, 0.0)
    o_intra = jnp.einsum('bhncd,bhndv->bhncv', scores, vc)
    q_dec = qc * jnp.exp(b)
    k_dec = kc * jnp.exp(b_last - b)
    decay = jnp.exp(b_last[:, :, :, 0, :])

    def step(state, inp):
        qd, kd, vv, dec = inp
        o = jnp.einsum('bhck,bhkv->bhcv', qd, state)
        state = state * dec[..., None] + jnp.einsum('bhck,bhcv->bhkv', kd, vv)
        return state, o

    xs = (jnp.moveaxis(q_dec, 2, 0), jnp.moveaxis(k_dec, 2, 0),
          jnp.moveaxis(vc, 2, 0), jnp.moveaxis(decay, 2, 0))
    state0 = jnp.zeros((b_, h_, k_, v_), jnp.float32)
    _, o_inter = lax.scan(step, state0, xs)
    o = o_intra + jnp.moveaxis(o_inter, 0, 2)
    return o.transpose(0, 2, 3, 1, 4).reshape(b_, s_, h_, v_).astype(out_dtype)


def hgrn2_gla_mixer(h, w_in, lb_logits, a_idx, hgrn_gain, w_gk, b_gk, gla_gain, w_out):
    b_, s_, _ = h.shape
    p = h @ w_in
    cuts = list(np.cumsum(AB_SPLITS)[:-1])
    a_q, a_f, a_i, a_g, g_q, g_k, g_v, g_g, g_low = jnp.split(p, cuts, axis=-1)

    lb = jnp.cumsum(jax.nn.softmax(lb_logits.astype(jnp.float32), axis=0), axis=0)[a_idx]
    f = lb + (1.0 - lb) * jax.nn.sigmoid(a_f.astype(jnp.float32))
    shp_k = (b_, s_, HGRN_HEADS, HGRN_HEAD_K)
    shp_v = (b_, s_, HGRN_HEADS, HGRN_HEAD_V)
    o_a = chunked_gated_linear_attention(
        jax.nn.silu(a_q).reshape(shp_k),
        (1.0 - f).reshape(shp_k),
        a_i.reshape(shp_v),
        jnp.log(f).reshape(shp_k))
    o_a = rms_norm(o_a, hgrn_gain) * jax.nn.sigmoid(a_g).reshape(shp_v)

    gk = jax.nn.log_sigmoid(g_low.astype(jnp.float32) @ w_gk.astype(jnp.float32)
                            + b_gk.astype(jnp.float32)) / GLA_GATE_NORMALIZER
    shp_gk = (b_, s_, GLA_HEADS, GLA_HEAD_K)
    shp_gv = (b_, s_, GLA_HEADS, GLA_HEAD_V)
    o_b = chunked_gated_linear_attention(
        g_q.reshape(shp_gk) * (GLA_HEAD_K ** -0.5),
        g_k.reshape(shp_gk),
        g_v.reshape(shp_gv),
        gk.reshape(shp_gk))
    o_b = rms_norm(o_b, gla_gain) * jax.nn.silu(g_g).reshape(shp_gv)

    o = jnp.concatenate([o_a.reshape(b_, s_, HGRN_WIDTH), o_b.reshape(b_, s_, GLA_WIDTH)], axis=-1)
    return o @ w_out


def stick_breaking_mixer(h, w_qkv, w_out):
    b_, s_, _ = h.shape
    qkv = (h @ w_qkv).reshape(b_, s_, 3, SB_HEADS, SB_HEAD_DIM).transpose(2, 0, 3, 1, 4)
    q, k, v = qkv[0], qkv[1], qkv[2]
    scale = SB_HEAD_DIM ** -0.5
    outs = []
    for blk in range(s_ // SB_QBLOCK):
        t0 = blk * SB_QBLOCK
        t1 = t0 + SB_QBLOCK
        z = jnp.einsum('bhtd,bhsd->bhts', q[:, :, t0:t1], k[:, :, :t1]).astype(jnp.float32) * scale
        t_idx = t0 + jnp.arange(SB_QBLOCK)
        s_idx = jnp.arange(t1)
        mask = s_idx[None, :] < t_idx[:, None]
        sp = jnp.where(mask, jax.nn.softplus(z), 0.0)
        log_a = z - lax.cumsum(sp, axis=3, reverse=True)
        att = jnp.exp(jnp.where(mask, log_a, -jnp.inf))
        outs.append(jnp.einsum('bhts,bhsd->bhtd', att.astype(v.dtype), v[:, :, :t1]))
    o = jnp.concatenate(outs, axis=2)
    return o.transpose(0, 2, 1, 3).reshape(b_, s_, D_MODEL) @ w_out


def memory_cross_attention(h, mem_n, w_q, w_kv, w_o):
    b_, s_, _ = h.shape
    m_ = mem_n.shape[1]
    q = (h @ w_q).reshape(b_, s_, XA_HEADS, XA_HEAD_DIM)
    kv = (mem_n @ w_kv).reshape(b_, m_, 2, XA_HEADS, XA_HEAD_DIM)
    k, v = kv[:, :, 0], kv[:, :, 1]
    scores = jnp.einsum('bshd,bmhd->bhsm', q, k).astype(jnp.float32) * (XA_HEAD_DIM ** -0.5)
    probs = jax.nn.softmax(scores, axis=-1).astype(v.dtype)
    o = jnp.einsum('bhsm,bmhd->bshd', probs, v).reshape(b_, s_, D_MODEL)
    return o @ w_o


def conv_ffn(h, w_in, conv_w, conv_b, w_out):
    s_ = h.shape[1]
    u, g = jnp.split(h @ w_in, 2, axis=-1)
    gp = jnp.pad(g, ((0, 0), (CONV_W - 1, 0), (0, 0)))
    gc = conv_b + sum(conv_w[j] * gp[:, j:j + s_] for j in range(CONV_W))
    return (jax.nn.silu(gc) * u) @ w_out


def setup_inputs(seed: int = 0) -> dict:
    key = jax.random.key(seed)
    ks = jax.random.split(key, 24)

    def nrm(k, shape, fan_in):
        return jax.random.normal(k, shape, jnp.float32) * (fan_in ** -0.5)

    def gain(k, shape):
        return 1.0 + 0.02 * jax.random.normal(k, shape, jnp.float32)

    return {
        "x": jax.random.normal(ks[0], (BATCH, SEQ, D_MODEL), jnp.float32),
        "mem": jax.random.normal(ks[1], (BATCH, MEM_LEN, D_MODEL), jnp.float32),
        "mem_norm": gain(ks[2], (D_MODEL,)),
        "norm_mix": gain(ks[3], (DEPTH, D_MODEL)),
        "norm_xattn": gain(ks[4], (DEPTH, D_MODEL)),
        "norm_ffn": gain(ks[5], (DEPTH, D_MODEL)),
        "ab_w_in": nrm(ks[6], (N_AB, D_MODEL, AB_IN), D_MODEL),
        "hgrn_lb_logits": jax.random.normal(ks[7], (N_AB + 1, HGRN_HEADS * HGRN_HEAD_K), jnp.float32),
        "hgrn_norm": gain(ks[8], (N_AB, HGRN_HEAD_V)),
        "gla_w_gk": nrm(ks[9], (N_AB, GLA_GATE_RANK, GLA_HEADS * GLA_HEAD_K), GLA_GATE_RANK),
        "gla_b_gk": 0.01 * jax.random.normal(ks[10], (N_AB, GLA_HEADS * GLA_HEAD_K), jnp.float32),
        "gla_norm": gain(ks[11], (N_AB, GLA_HEAD_V)),
        "ab_w_out": nrm(ks[12], (N_AB, D_MODEL, D_MODEL), D_MODEL),
        "sb_w_qkv": nrm(ks[13], (N_C, D_MODEL, 3 * D_MODEL), D_MODEL),
        "sb_w_out": nrm(ks[14], (N_C, D_MODEL, D_MODEL), D_MODEL),
        "xa_w_q": nrm(ks[15], (DEPTH, D_MODEL, D_MODEL), D_MODEL),
        "xa_w_kv": nrm(ks[16], (DEPTH, D_MODEL, 2 * D_MODEL), D_MODEL),
        "xa_w_o": nrm(ks[17], (DEPTH, D_MODEL, D_MODEL), D_MODEL),
        "ffn_w_in": nrm(ks[18], (DEPTH, D_MODEL, 2 * D_FF), D_MODEL),
        "ffn_conv_w": nrm(ks[19], (DEPTH, CONV_W, D_FF), CONV_W),
        "ffn_conv_b": 0.01 * jax.random.normal(ks[20], (DEPTH, D_FF), jnp.float32),
        "ffn_w_out": nrm(ks[21], (DEPTH, D_FF, D_MODEL), D_FF),
        "final_norm": gain(ks[22], (D_MODEL,)),
    }


def reference(x, mem, mem_norm, norm_mix, norm_xattn, norm_ffn, ab_w_in, hgrn_lb_logits,
              hgrn_norm, gla_w_gk, gla_b_gk, gla_norm, ab_w_out, sb_w_qkv, sb_w_out,
              xa_w_q, xa_w_kv, xa_w_o, ffn_w_in, ffn_conv_w, ffn_conv_b, ffn_w_out, final_norm):
    mem_n = rms_norm(mem, mem_norm)
    for layer in range(DEPTH):
        h = rms_norm(x, norm_mix[layer])
        if layer % 2 == 0:
            a = layer // 2
            x = x + hgrn2_gla_mixer(h, ab_w_in[a], hgrn_lb_logits, a, hgrn_norm[a],
                                    gla_w_gk[a], gla_b_gk[a], gla_norm[a], ab_w_out[a])
        else:
            c = layer // 2
            x = x + stick_breaking_mixer(h, sb_w_qkv[c], sb_w_out[c])
        x = x + memory_cross_attention(rms_norm(x, norm_xattn[layer]), mem_n,
                                       xa_w_q[layer], xa_w_kv[layer], xa_w_o[layer])
        x = x + conv_ffn(rms_norm(x, norm_ffn[layer]), ffn_w_in[layer], ffn_conv_w[layer],
                         ffn_conv_b[layer], ffn_w_out[layer])
    return rms_norm(x, final_norm)
```

```python
import math
from contextlib import ExitStack

import numpy as np
import concourse.bass as bass
import concourse.mybir as mybir
from concourse.bass_utils import run_bass_kernel_spmd

F32 = mybir.dt.float32
BF16 = mybir.dt.bfloat16
AF = mybir.ActivationFunctionType
ALU = mybir.AluOpType

T = 2048
D = 2048
DFF = 5632
MEM = 256
ENGS = ("pe", "act", "dve", "pool", "sp")


class KAP:
    def __init__(self, ap, keys):
        self.ap, self.keys = ap, list(keys)


def _ap(a):
    return a.ap if isinstance(a, KAP) else a


def _keys(a):
    if a is None or isinstance(a, (int, float)):
        return []
    if isinstance(a, KAP):
        return list(a.keys)
    if "DRAM" in str(a.space):
        return []
    return [a.name]


class Prog:
    def __init__(self, nc):
        self.nc = nc
        self.ops = []
        self.eng = {"pe": nc.tensor, "act": nc.scalar, "dve": nc.vector, "pool": nc.gpsimd, "sp": nc.sync}

    def op(self, eng, fn, reads, writes, dma=None):
        self.ops.append((eng, fn, tuple(reads), tuple(writes), dma))

    def barrier(self):
        self.ops.append(("barrier",))

    def mm(self, out, lhsT, rhs, start=True, stop=True, **kw):
        o, l, r = _ap(out), _ap(lhsT), _ap(rhs)
        self.op("pe", lambda E: E.matmul(o, l, r, start=start, stop=stop, **kw), _keys(lhsT) + _keys(rhs), _keys(out))

    def transpose(self, out, in_, ident):
        o, i, d = _ap(out), _ap(in_), _ap(ident)
        self.op("pe", lambda E: E.transpose(o, i, d), _keys(in_) + _keys(ident), _keys(out))

    def act(self, out, in_, func, bias=None, scale=1.0, accum_out=None, eng="act"):
        o, i = _ap(out), _ap(in_)
        b = _ap(bias) if bias is not None else None
        s = _ap(scale)
        a = _ap(accum_out) if accum_out is not None else None
        kw = {}
        if b is not None:
            kw["bias"] = b
        if a is not None:
            kw["accum_out"] = a
        self.op(eng, lambda E: E.activation(out=o, in_=i, func=func, scale=s, **kw),
                _keys(in_) + _keys(bias) + _keys(scale), _keys(out) + _keys(accum_out))

    def tt(self, out, in0, in1, op, eng="dve"):
        o, a, b = _ap(out), _ap(in0), _ap(in1)
        self.op(eng, lambda E: E.tensor_tensor(o, a, b, op), _keys(in0) + _keys(in1), _keys(out))

    def ts(self, out, in0, s1, s2, op0, op1=None, eng="dve"):
        o, a, x1, x2 = _ap(out), _ap(in0), _ap(s1), _ap(s2)
        if op1 is None:
            fn = lambda E: E.tensor_scalar(o, a, x1, None, op0)
        else:
            fn = lambda E: E.tensor_scalar(o, a, x1, x2, op0, op1)
        self.op(eng, fn, _keys(in0) + _keys(s1) + _keys(s2), _keys(out))

    def stt(self, out, in0, scalar, in1, op0, op1, eng="dve"):
        o, a, s, b = _ap(out), _ap(in0), _ap(scalar), _ap(in1)
        self.op(eng, lambda E: E.scalar_tensor_tensor(o, a, s, b, op0, op1),
                _keys(in0) + _keys(scalar) + _keys(in1), _keys(out))

    def copy(self, out, in_, eng="dve"):
        o, i = _ap(out), _ap(in_)
        if eng == "act":
            self.op(eng, lambda E: E.activation(out=o, in_=i, func=AF.Copy), _keys(in_), _keys(out))
        else:
            self.op(eng, lambda E: E.tensor_copy(o, i), _keys(in_), _keys(out))

    def scan(self, out, d0, d1, init, op0, op1):
        o, a, b = _ap(out), _ap(d0), _ap(d1)
        self.op("dve", lambda E: E.tensor_tensor_scan(o, a, b, init, op0, op1), _keys(d0) + _keys(d1), _keys(out))

    def recip(self, out, in_):
        o, i = _ap(out), _ap(in_)
        self.op("dve", lambda E: E.reciprocal(o, i), _keys(in_), _keys(out))

    def memset(self, out, val, eng="dve"):
        o = _ap(out)
        self.op(eng, lambda E: E.memset(o, val), [], _keys(out))

    def aselect(self, out, in_, cmp, fill, base, pattern, cm):
        o, i = _ap(out), _ap(in_)
        self.op("pool", lambda E: E.affine_select(out=o, in_=i, compare_op=cmp, fill=fill, base=base,
                                                    pattern=pattern, channel_multiplier=cm), _keys(in_), _keys(out))

    def dma(self, eng, out, in_, chan, rk=(), wk=(), **kw):
        o, i = _ap(out), _ap(in_)
        self.op(eng, lambda E: E.dma_start(out=o, in_=i, **kw), _keys(in_) + list(rk), _keys(out) + list(wk), dma=chan)

    def finish(self, stack):
        nc = self.nc
        ops = self.ops
        n = len(ops)
        last_w = {}
        readers = {}
        deps = [None] * n
        need_inc = [False] * n
        last_on = {}
        pend = {}
        for i, o in enumerate(ops):
            if o[0] == "barrier":
                allidx = list(last_on.values())
                for e in ENGS:
                    pend[e] = list(allidx)
                continue
            eng, fn, R, W, dma = o
            d = {}
            for k in R:
                j = last_w.get(k)
                if j is not None:
                    d[j] = "raw"
            for k in W:
                j = last_w.get(k)
                if j is not None and j not in d:
                    d[j] = "waw"
                for j in readers.get(k, {}).values():
                    if j not in d:
                        d[j] = "war"
            if eng in pend:
                for j in pend.pop(eng):
                    d[j] = "raw"
            keep = []
            for j, kind in d.items():
                if j == i:
                    continue
                oj = ops[j]
                if oj[4] is None and oj[0] == eng and dma is None:
                    if eng == "pe":
                        continue
                keep.append(j)
            keep.sort()
            deps[i] = keep
            for j in keep:
                need_inc[j] = True
            src = dma if dma is not None else eng
            for k in W:
                last_w[k] = i
                readers[k] = {}
            for k in R:
                readers.setdefault(k, {})[src] = i
            last_on[src] = i
        sem_eng = {e: stack.enter_context(nc.semaphore("S_" + e)) for e in ("pe", "act", "dve", "pool")}
        cnt_eng = {e: 0 for e in sem_eng}
        chan_sem = {}
        chan_cnt = {}
        token = [None] * n
        waited = {e: {} for e in ENGS}
        self.trace = {e: [] for e in ENGS}
        for i, o in enumerate(ops):
            if o[0] == "barrier":
                continue
            eng, fn, R, W, dma = o
            E = self.eng[eng]
            myw = []
            for j in deps[i]:
                oj = ops[j]
                if oj[4] is not None:
                    sem = chan_sem[oj[4]]
                    val = chan_cnt[oj[4]]
                else:
                    sem, val = token[j]
                if waited[eng].get(sem.name, 0) >= val:
                    continue
                E.wait_ge(sem, val)
                waited[eng][sem.name] = val
                myw.append((sem.name, val))
            ins = fn(E)
            if dma is not None:
                if dma not in chan_sem:
                    chan_sem[dma] = stack.enter_context(nc.semaphore("C_" + dma))
                    chan_cnt[dma] = 0
                chan_cnt[dma] += 16
                ins.then_inc(chan_sem[dma], 16)
                self.trace[eng].append((myw, (chan_sem[dma].name, 16), i))
            elif need_inc[i]:
                cnt_eng[eng] += 1
                ins.then_inc(sem_eng[eng], 1)
                token[i] = (sem_eng[eng], cnt_eng[eng])
                self.trace[eng].append((myw, (sem_eng[eng].name, 1), i))
            else:
                self.trace[eng].append((myw, None, i))
        E = self.eng["sp"]
        for c, sem in chan_sem.items():
            if waited["sp"].get(sem.name, 0) < chan_cnt[c]:
                E.wait_ge(sem, chan_cnt[c])
        self.stats = dict(n_ops=n, cnt_eng=dict(cnt_eng), n_chan=len(chan_sem))

    def simulate(self):
        sems = {}
        pos = {e: 0 for e in ENGS}
        progress = True
        while progress:
            progress = False
            for e in ENGS:
                tr = self.trace[e]
                while pos[e] < len(tr):
                    waits, inc, i = tr[pos[e]]
                    if any(sems.get(sn, 0) < v for sn, v in waits):
                        break
                    if inc is not None:
                        sems[inc[0]] = sems.get(inc[0], 0) + inc[1]
                    pos[e] += 1
                    progress = True
        stuck = {e: (pos[e], len(self.trace[e])) for e in ENGS if pos[e] < len(self.trace[e])}
        info = {}
        for e in stuck:
            waits, inc, i = self.trace[e][pos[e]]
            info[e] = dict(op=i, waits=[(sn, v, sems.get(sn, 0)) for sn, v in waits])
        return stuck, info


class Phase:
    _id = 0

    def __init__(self, P, name):
        self.P, self.nc = P, P.nc
        Phase._id += 1
        self.pfx = f"{name}{Phase._id}_"
        self.stack = ExitStack()

    def __enter__(self):
        self.stack.__enter__()
        return self

    def __exit__(self, *a):
        self.P.barrier()
        return self.stack.__exit__(*a)

    def sb(self, name, shape, dt):
        return self.stack.enter_context(self.nc.sbuf_tensor(self.pfx + name, list(shape), dt))

    def ps(self, name, shape, dt=F32):
        return self.stack.enter_context(self.nc.psum_tensor(self.pfx + name, list(shape), dt))


def wview(W2d, c0, n):
    return W2d[:, c0:c0 + n].rearrange("(kc p) f -> p kc f", p=128)


def build(dbg=None):
    dbg = dbg or {}
    stop_after = dbg.get("stop_after")
    dump = set(dbg.get("dump", ()))
    nc = bass.Bass("TRN2", target_bir_lowering=False)
    P = Prog(nc)
    top = ExitStack()

    def din(name, shape):
        return nc.dram_tensor(name, list(shape), F32, kind="ExternalInput").ap()

    x_in = din("x", [T, D])
    mem_in = din("mem", [MEM, D])
    mem_norm = din("mem_norm", [D])
    norm_mix = din("norm_mix", [2, D])
    norm_xattn = din("norm_xattn", [2, D])
    norm_ffn = din("norm_ffn", [2, D])
    ab_w_in = din("ab_w_in", [1, D, 7184])
    hgrn_lb_logits = din("hgrn_lb_logits", [2, 1024])
    hgrn_norm = din("hgrn_norm", [1, 128])
    gla_w_gk = din("gla_w_gk", [1, 16, 512])
    gla_b_gk = din("gla_b_gk", [1, 512])
    gla_norm = din("gla_norm", [1, 256])
    ab_w_out = din("ab_w_out", [1, D, D])
    sb_w_qkv = din("sb_w_qkv", [1, D, 3 * D])
    sb_w_out = din("sb_w_out", [1, D, D])
    xa_w_q = din("xa_w_q", [2, D, D])
    xa_w_kv = din("xa_w_kv", [2, D, 2 * D])
    xa_w_o = din("xa_w_o", [2, D, D])
    ffn_w_in = din("ffn_w_in", [2, D, 2 * DFF])
    ffn_conv_w = din("ffn_conv_w", [2, 3, DFF])
    ffn_conv_b = din("ffn_conv_b", [2, DFF])
    ffn_w_out = din("ffn_w_out", [2, DFF, D])
    final_norm = din("final_norm", [D])
    out = nc.dram_tensor("out", [T, D], F32, kind="ExternalOutput").ap()

    def scr(name, shape, dt):
        kind = "ExternalOutput" if name in dump else "Internal"
        return nc.dram_tensor(name, list(shape), dt, kind=kind).ap()

    xs = scr("xs", [T, D], F32)
    actA = scr("actA", [D, T], BF16)
    ffa = scr("ffa", [DFF, T], BF16)
    sbq = scr("sbq", [D, T], BF16)
    sbk = scr("sbk", [D, T], BF16)
    sbv = scr("sbv", [T, D], BF16)
    xaK = [scr(f"xaK{l}", [D, MEM], BF16) for l in range(2)]
    xaV = [scr(f"xaV{l}", [MEM, D], BF16) for l in range(2)]

    def gsb(name, shape, dt):
        return top.enter_context(nc.sbuf_tensor(name, list(shape), dt))

    ident = gsb("ident", [128, 128], BF16)
    ones_f = gsb("ones_f", [128, 128], F32)
    ones_b = gsb("ones_b", [128, 128], BF16)
    negtri = gsb("negtri", [128, 128], F32)
    negones = gsb("negones", [128, 128], F32)
    pmask = gsb("pmask", [128, 128], F32)
    rmask = gsb("rmask", [128, 512], F32)
    lgt = gsb("lgt", [128, 2, 8], F32)
    lb = gsb("lb", [128, 8], F32)
    oml = gsb("oml", [128, 8], F32)
    lnoml = gsb("lnoml", [128, 8], F32)
    hgain = gsb("hgain", [128, 1], F32)
    ggain = gsb("ggain", [128, 2], F32)
    negb = gsb("negb", [128, 4], F32)
    wgk = gsb("wgk", [16, 512], F32)
    cw = gsb("cw", [128, 2, 3, 44], F32)
    cb = gsb("cb", [128, 2, 44], F32)

    P.memset(ident[:], 0.0, eng="pool")
    P.aselect(ident[:], ident[:], ALU.not_equal, 1.0, 0, [[-1, 128]], 1)
    P.memset(ones_f[:], 1.0, eng="pool")
    P.memset(ones_b[:], 1.0, eng="pool")
    P.memset(negones[:], -1.0, eng="pool")
    P.memset(negtri[:], -1.0, eng="pool")
    P.aselect(negtri[:], negtri[:], ALU.is_ge, 0.0, 0, [[-1, 128]], 1)
    P.memset(pmask[:], 1.0, eng="pool")
    P.aselect(pmask[:], pmask[:], ALU.is_ge, 0.0, 0, [[1, 128]], -1)
    P.memset(pmask[0:64, 64:128], 0.0, eng="pool")
    P.memset(rmask[:], 1.0, eng="pool")
    P.memset(rmask[:].rearrange("p (c t) -> p c t", t=64)[:, :, 0:1], 0.0, eng="pool")
    NCD = dict(allow_slow_non_contiguous=True)
    P.dma("sp", lgt[:], hgrn_lb_logits.rearrange("r (h p) -> p r h", p=128), chan="c0", **NCD)
    P.dma("sp", hgain[:], hgrn_norm.rearrange("o p -> p o"), chan="c0", **NCD)
    P.dma("sp", ggain[:], gla_norm[0].rearrange("(c p) -> p c", p=128), chan="c0", **NCD)
    P.dma("sp", negb[:], gla_b_gk[0].rearrange("(h p) -> p h", p=128), chan="c0", **NCD)
    P.dma("sp", wgk[:], gla_w_gk[0], chan="c0")
    for l in range(2):
        for j in range(3):
            P.dma("sp", cw[:, l, j, :], ffn_conv_w[l, j].rearrange("(c p) -> p c", p=128), chan="c0", **NCD)
        P.dma("sp", cb[:, l, :], ffn_conv_b[l].rearrange("(c p) -> p c", p=128), chan="c0", **NCD)
    P.tt(lb[:], lgt[:, 0, :], lgt[:, 1, :], ALU.subtract)
    P.act(lb[:], lb[:], AF.Sigmoid)
    P.ts(oml[:], lb[:], -1.0, 1.0, ALU.mult, ALU.add)
    P.act(lnoml[:], oml[:], AF.Ln)
    P.ts(negb[:], negb[:], -1.0, None, ALU.mult)
    P.barrier()

    def stopped(name):
        return stop_after is not None and stop_after == name

    def norm_T(name, x_fn, gain_ap, actT, ntiles):
        with Phase(P, name) as ph:
            gbc = ph.sb("gbc", [128, D], F32)
            P.dma("sp", gbc[:], gain_ap.partition_broadcast(128), chan="gbc")
            NX = 4
            xt = [ph.sb(f"xt{k}", [128, D], F32) for k in range(NX)]
            hb = [ph.sb(f"hb{k}", [128, D], BF16) for k in range(NX)]
            junk = ph.sb("junk", [128, D], BF16)
            ss = [ph.sb(f"ss{k}", [128, 1], F32) for k in range(NX)]
            rs = [ph.sb(f"rs{k}", [128, 1], F32) for k in range(NX)]
            tp = [ph.ps(f"tp{k}", [128, 8, 128], BF16) for k in range(4)]
            def st1(i):
                b = i % NX
                P.dma("sp", xt[b][:], x_fn(i), chan=f"xt{b}")
                P.act(junk[:], xt[b][:], AF.Square, accum_out=ss[b][:])
                P.act(rs[b][:], ss[b][:], AF.Sqrt, bias=1e-6, scale=1.0 / D)
                P.recip(rs[b][:], rs[b][:])
                P.stt(hb[b][:], xt[b][:], rs[b][:], gbc[:], ALU.mult, ALU.mult)

            def st2(i):
                b = i % NX
                for half in range(2):
                    t = tp[(i * 2 + half) % 4]
                    for c in range(8):
                        P.transpose(t[:, c, :], hb[b][:, (half * 8 + c) * 128:(half * 8 + c + 1) * 128], ident[:])
                    P.copy(actT[:, half * 8:(half + 1) * 8, i * 128:(i + 1) * 128], t[:],
                           eng="act" if half == 0 else "dve")

            st1(0)
            for i in range(ntiles):
                if i + 1 < ntiles:
                    st1(i + 1)
                st2(i)

    def load_w(wb, W2d, specs, KC, slot=0):
        for (d0, s0, n) in specs:
            P.dma("pool", wb[:, 0:KC, d0:d0 + n], wview(W2d, s0, n), chan=f"w{slot}")

    class WStream:
        def __init__(self, wb, loads):
            self.wb, self.loads, self.issued = wb, loads, 0

        def _issue(self, k):
            while self.issued <= k and self.issued < len(self.loads):
                W2d, specs, KC = self.loads[self.issued]
                load_w(self.wb[self.issued % 2], W2d, specs, KC, slot=self.issued % 2)
                self.issued += 1

        def get(self, k):
            self._issue(k + 1)
            return self.wb[k % 2]

    def load_actT(actT, src, KC, c0=0, ncols=T):
        for g in range(0, KC, 4):
            ge = min(KC, g + 4)
            P.dma("sp", KAP(actT[:, g:ge, 0:ncols], [f"{actT.name}.g{g // 4}"]),
                  src[g * 128:ge * 128, c0:c0 + ncols].rearrange("(kc p) t -> p kc t", p=128), chan=f"actT{(g // 4) % 4}")

    def outproj(name, src, K, W2d, x_src, x_dst):
        KC = K // 128
        G = 1 if KC <= 16 else 2
        FT = 512
        NF = D // FT
        TG = T // G
        with Phase(P, name) as ph:
            actT = ph.sb("actT", [128, KC, TG], BF16)
            wb = [ph.sb(f"wb{k}", [128, KC, FT], BF16) for k in range(2)]
            xin = [ph.sb(f"xin{k}", [128, FT], F32) for k in range(4)]
            ps = [ph.ps(f"ps{k}", [128, FT], F32) for k in range(4)]
            ws = WStream(wb, [(W2d, [(0, f * FT, FT)], KC) for g in range(G) for f in range(NF)])
            units = [(g, f, i) for g in range(G) for f in range(NF) for i in range(TG // 128)]

            def ldx(u):
                g, f, i = units[u]
                r0 = g * TG + i * 128
                P.dma("sp", xin[u % 4][:], x_src[r0:r0 + 128, f * FT:(f + 1) * FT], chan=f"xin{u % 4}")

            PF = int(dbg.get("pf", 2))
            for u in range(min(PF, len(units))):
                ldx(u)
            for u, (g, f, i) in enumerate(units):
                if f == 0 and i == 0:
                    load_actT(actT, src, KC, g * TG, TG)
                w = ws.get(g * NF + f)
                r0 = g * TG + i * 128
                p = ps[u % 4]
                xi = xin[u % 4]
                if u + PF < len(units):
                    ldx(u + PF)
                for kc in range(KC):
                    P.mm(p[:], KAP(actT[:, kc, i * 128:(i + 1) * 128], [f"{actT.name}.g{kc // 4}"]), w[:, kc, :],
                         start=kc == 0, stop=kc == KC - 1)
                P.tt(xi[:], xi[:], p[:], ALU.add)
                P.dma("act", x_dst[r0:r0 + 128, f * FT:(f + 1) * FT], xi[:], chan=f"xin{u % 4}")

    def mem_phase():
        with Phase(P, "mem") as ph:
            memT = ph.sb("memT", [128, 16, MEM], BF16)
            w32 = [ph.sb(f"w32_{k}", [128, 16, 512], F32) for k in range(2)]
            wb = [ph.sb(f"wb{k}", [128, 16, 512], BF16) for k in range(2)]
            loads = [(xa_w_kv[l], half * D + ft * 512) for l in range(2) for half in range(2) for ft in range(4)]
            stt_ = dict(issued=0)

            def issue_upto(k):
                while stt_["issued"] <= k and stt_["issued"] < len(loads):
                    j = stt_["issued"]
                    W2d_, c0 = loads[j]
                    src = wview(W2d_, c0, 512)
                    P.dma("sp", w32[j % 2][:, 0:8, :], src[:, 0:8, :], chan=f"w{j % 2}")
                    P.dma("sp", w32[j % 2][:, 8:16, :], src[:, 8:16, :], chan=f"w{j % 2}")
                    stt_["issued"] += 1

            def getw(k):
                issue_upto(k)
                P.copy(wb[k % 2][:, 0:8, :], w32[k % 2][:, 0:8, :], eng="dve")
                P.act(wb[k % 2][:, 8:16, :], w32[k % 2][:, 8:16, :], AF.Copy)
                issue_upto(k + 2)
                return wb[k % 2]

            issue_upto(1)
            norm_T("memn", lambda i: mem_in[i * 128:(i + 1) * 128, :], mem_norm, memT, 2)
            st = [ph.sb(f"st{k}", [128, 512], BF16) for k in range(4)]
            ps = [ph.ps(f"ps{k}", [128, 512], F32) for k in range(4)]
            u = 0
            wi = 0
            for l in range(2):
                for ft in range(4):
                    w = getw(wi)
                    wi += 1
                    for c in range(4):
                        p = ps[u % 4]
                        s = st[u % 4]
                        for kc in range(16):
                            P.mm(p[:, 0:MEM], w[:, kc, c * 128:(c + 1) * 128], memT[:, kc, :], start=kc == 0, stop=kc == 15)
                        P.copy(s[:, 0:MEM], p[:, 0:MEM], eng="act")
                        r0 = (ft * 4 + c) * 128
                        P.dma("pool", xaK[l][r0:r0 + 128, :], s[:, 0:MEM], chan=f"st{u % 4}")
                        u += 1
                for ft in range(4):
                    w = getw(wi)
                    wi += 1
                    for mb in range(2):
                        p = ps[u % 4]
                        s = st[u % 4]
                        for kc in range(16):
                            P.mm(p[:], memT[:, kc, mb * 128:(mb + 1) * 128], w[:, kc, :], start=kc == 0, stop=kc == 15)
                        P.copy(s[:], p[:], eng="dve")
                        P.dma("pool", xaV[l][mb * 128:(mb + 1) * 128, ft * 512:(ft + 1) * 512], s[:], chan=f"st{u % 4}")
                        u += 1

    def ab_phase(x_src):
        W2d = ab_w_in[0]
        with Phase(P, "ab") as ph:
            actT = ph.sb("actT", [128, 16, T], BF16)
            wb = [ph.sb(f"wb{k}", [128, 16, 768], BF16) for k in range(2)]
            wlow = ph.sb("wlow", [128, 16, 16], BF16)
            load_w(wlow, W2d, [(0, 7168, 16)], 16, slot=2)
            hl = []
            for hh_ in range(12):
                h_ = hh_ - 8 if hh_ >= 8 else hh_
                if hh_ >= 8:
                    hl.append((W2d, [(0, 4096 + h_ * 128, 128), (128, 4608 + h_ * 128, 128),
                                     (256, 5120 + h_ * 256, 256), (512, 6144 + h_ * 256, 256)], 16))
                else:
                    hl.append((W2d, [(0, h_ * 128, 128), (128, 1024 + h_ * 128, 128),
                                     (256, 2048 + h_ * 128, 128), (384, 3072 + h_ * 128, 128)], 16))
            ws = WStream(wb, hl)
            ws._issue(1)
            norm_T("abn", lambda i: x_src[i * 128:(i + 1) * 128, :], norm_mix[0], actT, 16)
            glow = ph.sb("glow", [16, T], F32)
            f32t = lambda nm: ph.sb(nm, [128, 512], F32)
            bft = lambda nm: ph.sb(nm, [128, 512], BF16)
            Qb = [f32t(f"Q{k}") for k in range(2)]
            Kb = [f32t(f"K{k}") for k in range(2)]
            LG = [f32t(f"LG{k}") for k in range(2)]
            GA = [[f32t(f"G{k}_{v}") for v in range(2)] for k in range(4)]
            VT = [ph.sb(f"vt{k}", [128, 4, 256], BF16) for k in range(3)]
            bb, d1, d2, exA, exB, ptmp, ptmp2 = [f32t(n_) for n_ in ("bb", "d1", "d2", "exA", "exB", "ptmp", "ptmp2")]
            qs, ks, kd = bft("qs"), bft("ks"), bft("kd")
            QD = [bft(f"qd{k}") for k in range(2)]
            KDT = [ph.sb(f"kdt{k}", [128, 4, 128], BF16) for k in range(2)]
            DEC = [ph.sb(f"dec{k}", [128, 8], F32) for k in range(2)]
            SCB = [ph.sb(f"scb{k}", [128, 4, 128], BF16) for k in range(2)]
            osb = [f32t("o0"), f32t("o1")]
            sqs = [f32t("sq0"), f32t("sq1")]
            rstd = f32t("rstd")
            S = ph.sb("S", [128, 256], F32)
            Sb = [ph.sb(f"Sb{k}", [128, 256], BF16) for k in range(2)]
            stg = [bft(f"stg{k}") for k in range(4)]
            PB = [ph.ps(f"pb{k}", [128, 512], F32) for k in range(3)]
            psc = ph.ps("psc", [128, 512], F32)
            po = [ph.ps(f"po{k}", [128, 512], F32) for k in range(2)]
            pds = ph.ps("pds", [128, 512], F32)
            ptr = ph.ps("ptr", [128, 4, 128], BF16)
            st = dict(pb=0, cur=0, stg=0)

            def nextpb():
                st["pb"] += 1
                return PB[st["pb"] % 3]

            for tt in range(4):
                p = nextpb()
                for kc in range(16):
                    P.mm(p[0:16, :], wlow[:, kc, :], actT[:, kc, tt * 512:(tt + 1) * 512], start=kc == 0, stop=kc == 15)
                P.copy(glow[:, tt * 512:(tt + 1) * 512], p[0:16, :], eng="act")
            NU = 48

            def info(u):
                hh, tt = divmod(u, 4)
                gla = hh >= 8
                h = hh - 8 if gla else hh
                return hh, tt, gla, h, (256 if gla else 128)

            def P_steps(u):
                hh, tt, gla, h, V = info(u)
                steps = []
                Q, K_, L = Qb[u % 2], Kb[u % 2], LG[u % 2]
                G = GA[u % 4]
                vt = VT[u % 3]
                tsl = slice(tt * 512, (tt + 1) * 512)

                def fgroup(c0, evac):
                    box = {}

                    def s0():
                        box["w"] = ws.get(hh)
                        box["p"] = nextpb()
                        for kc in range(8):
                            P.mm(box["p"][:], box["w"][:, kc, c0:c0 + 128], actT[:, kc, tsl], start=kc == 0, stop=False)

                    def s1():
                        for kc in range(8, 16):
                            P.mm(box["p"][:], box["w"][:, kc, c0:c0 + 128], actT[:, kc, tsl], start=False, stop=kc == 15)
                        evac(box["p"])
                    steps.append(s0)
                    steps.append(s1)

                if gla:
                    fgroup(0, lambda p: P.act(Q[:], p[:], AF.Copy, scale=128 ** -0.5))
                    fgroup(128, lambda p: P.copy(K_[:], p[:], eng="act"))

                    def gk():
                        p = nextpb()
                        P.mm(p[:], wgk[0:16, h * 128:(h + 1) * 128], glow[:, tsl], start=True, stop=True)
                        P.act(ptmp[:], p[:], AF.Exp, bias=negb[:, h:h + 1], scale=-1.0)
                        P.act(ptmp[:], ptmp[:], AF.Ln, bias=1.0)
                        P.ts(L[:], ptmp[:], -1.0 / 16.0, None, ALU.mult)
                    steps.append(gk)
                    def silu_ev(p, dst):
                        P.act(ptmp2[:], p[:], AF.Exp, scale=-1.0)
                        P.act(ptmp2[:], ptmp2[:], AF.Ln, bias=1.0)
                        P.act(ptmp2[:], ptmp2[:], AF.Exp, scale=-1.0)
                        P.tt(dst[:], ptmp2[:], p[:], ALU.mult)
                    for vc in range(2):
                        fgroup(512 + vc * 128, lambda p, vc=vc: silu_ev(p, G[vc]))
                    for hf in range(2):
                        def vhalf(hf=hf):
                            w = ws.get(hh)
                            p = nextpb()
                            for jj in range(2):
                                j = hf * 2 + jj
                                for kc in range(16):
                                    P.mm(p[:, jj * 256:(jj + 1) * 256], actT[:, kc, tt * 512 + j * 128: tt * 512 + (j + 1) * 128],
                                         w[:, kc, 256:512], start=kc == 0, stop=kc == 15)
                            P.copy(vt[:, hf * 2:hf * 2 + 2, :], p[:].rearrange("p (j v) -> p j v", v=256), eng="dve")
                        steps.append(vhalf)
                else:
                    def silu_ev(p, dst):
                        P.act(ptmp2[:], p[:], AF.Exp, scale=-1.0)
                        P.act(ptmp2[:], ptmp2[:], AF.Ln, bias=1.0)
                        P.act(ptmp2[:], ptmp2[:], AF.Exp, scale=-1.0)
                        P.tt(dst[:], ptmp2[:], p[:], ALU.mult)
                    fgroup(0, lambda p: silu_ev(p, Q))

                    def fev(p):
                        P.act(ptmp[:], p[:], AF.Exp, scale=-1.0)
                        P.act(L[:], ptmp[:], AF.Ln, bias=1.0, scale=lb[:, h:h + 1])
                        P.act(ptmp[:], ptmp[:], AF.Ln, bias=1.0)
                        P.tt(L[:], L[:], ptmp[:], ALU.subtract)
                        P.stt(ptmp[:], p[:], -1.0, ptmp[:], ALU.mult, ALU.subtract)
                        P.act(K_[:], ptmp[:], AF.Exp, bias=lnoml[:, h:h + 1])
                    fgroup(128, fev)

                    def sig_ev(p):
                        P.act(ptmp2[:], p[:], AF.Exp, scale=-1.0)
                        P.act(ptmp2[:], ptmp2[:], AF.Ln, bias=1.0)
                        P.act(G[0][:], ptmp2[:], AF.Exp, scale=-1.0)
                    fgroup(384, sig_ev)
                    for hf in range(2):
                        def vhalf(hf=hf):
                            w = ws.get(hh)
                            if hf == 0:
                                st["pv"] = nextpb()
                            p = st["pv"]
                            for jj in range(2):
                                j = hf * 2 + jj
                                for kc in range(16):
                                    P.mm(p[:, j * 128:(j + 1) * 128], actT[:, kc, tt * 512 + j * 128: tt * 512 + (j + 1) * 128],
                                         w[:, kc, 256:384], start=kc == 0, stop=kc == 15)
                            if hf == 1:
                                P.copy(vt[:, :, 0:128], p[:].rearrange("p (j v) -> p j v", v=128), eng="dve")
                        steps.append(vhalf)
                return steps

            def E_steps(u):
                hh, tt, gla, h, V = info(u)
                Q, K_, L = Qb[u % 2], Kb[u % 2], LG[u % 2]
                qd, kdt, dec, scb = QD[u % 2], KDT[u % 2], DEC[u % 2], SCB[u % 2]
                bv = bb[:].rearrange("p (c t) -> p c t", t=64)

                def e0():
                    P.scan(bb[:], rmask[:], L[:], 0.0, ALU.mult, ALU.add)
                    P.tt(d1[:].rearrange("p (c t) -> p c t", t=64), bv, bv[:, :, 31:32].to_broadcast([128, 8, 64]), ALU.subtract)
                    P.tt(d2[:].rearrange("p (c t) -> p c t", t=64), bv, bv[:, :, 63:64].to_broadcast([128, 8, 64]), ALU.subtract)
                    P.act(dec[:].rearrange("p (c o) -> p c o", o=1), bv[:, :, 63:64], AF.Exp)

                def e1():
                    P.act(exA[:], d1[:], AF.Exp)
                    P.tt(qs[:], Q[:], exA[:], ALU.mult)
                    P.act(exB[:], d1[:], AF.Exp, scale=-1.0)
                    P.tt(ks[:], K_[:], exB[:], ALU.mult)

                def e2():
                    P.act(exA[:], bb[:], AF.Exp)
                    P.tt(qd[:], Q[:], exA[:], ALU.mult)
                    P.act(exB[:], d2[:], AF.Exp, scale=-1.0)
                    P.tt(kd[:], K_[:], exB[:], ALU.mult)

                def e3():
                    for j in range(4):
                        P.transpose(ptr[:, j, :], kd[:, j * 128:(j + 1) * 128], ident[:])
                    P.copy(kdt[:], ptr[:], eng="act")

                def e4():
                    for j in range(4):
                        P.mm(psc[:, j * 128:(j + 1) * 128], ks[:, j * 128:(j + 1) * 128], qs[:, j * 128:(j + 1) * 128],
                             start=True, stop=True)
                    P.stt(scb[:], psc[:].rearrange("p (j t) -> p j t", t=128), 1e30,
                          pmask[:].rearrange("p (o t) -> p o t", o=1).to_broadcast([128, 4, 128]), ALU.min, ALU.mult)
                return [e0, e1, e2, e3, e4]

            def R_steps(u):
                hh, tt, gla, h, V = info(u)
                NV = V // 128
                qd, kdt, dec, scb, vt = QD[u % 2], KDT[u % 2], DEC[u % 2], SCB[u % 2], VT[u % 3]
                steps = []
                for c in range(8):
                    def step(c=c):
                        j, half = divmod(c, 2)
                        if c == 0 and tt == 0:
                            P.memset(S[:], 0.0)
                            st["cur"] = 0
                            P.memset(Sb[0][:], 0.0)
                        cur = st["cur"]
                        if half == 0:
                            for vc in range(NV):
                                P.mm(po[vc][:, j * 128:(j + 1) * 128], vt[:, j, vc * 128:(vc + 1) * 128], scb[:, j, :],
                                     start=True, stop=False, skip_group_check=True)
                        for vc in range(NV):
                            P.mm(po[vc][:, c * 64:(c + 1) * 64], Sb[cur][:, vc * 128:(vc + 1) * 128], qd[:, c * 64:(c + 1) * 64],
                                 start=False, stop=(half == 1), skip_group_check=True)
                        P.mm(pds[:, 0:V], kdt[half * 64:(half + 1) * 64, j, :], vt[half * 64:(half + 1) * 64, j, 0:V],
                             start=True, stop=True)
                        if dbg.get("sbcopy", "dve2") == "dve2":
                            P.stt(Sb[1 - cur][:, 0:V], S[:, 0:V], dec[:, c:c + 1], pds[:, 0:V], ALU.mult, ALU.add)
                            P.stt(S[:, 0:V], S[:, 0:V], dec[:, c:c + 1], pds[:, 0:V], ALU.mult, ALU.add)
                        else:
                            P.stt(S[:, 0:V], S[:, 0:V], dec[:, c:c + 1], pds[:, 0:V], ALU.mult, ALU.add)
                            P.copy(Sb[1 - cur][:, 0:V], S[:, 0:V], eng=dbg["sbcopy"])
                        st["cur"] = 1 - cur
                    steps.append(step)
                return steps

            def F_stage(u):
                hh, tt, gla, h, V = info(u)
                NV = V // 128
                G = GA[u % 4]
                for vc in range(NV):
                    P.copy(osb[vc][:], po[vc][:], eng="dve")
                    P.act(sqs[vc][:], po[vc][:], AF.Square)
                p = nextpb()
                for vc in range(NV):
                    P.mm(p[:], ones_f[:], sqs[vc][:], start=vc == 0, stop=vc == NV - 1)
                P.act(rstd[:], p[:], AF.Ln, bias=1e-6, scale=1.0 / V)
                P.act(rstd[:], rstd[:], AF.Exp, scale=-0.5)
                for vc in range(NV):
                    P.tt(osb[vc][:], osb[vc][:], rstd[:], ALU.mult)
                    gn = ggain[:, vc:vc + 1] if gla else hgain[:, 0:1]
                    sg = stg[st["stg"] % 4]
                    P.stt(sg[:], osb[vc][:], gn, G[vc][:], ALU.mult, ALU.mult)
                    r0 = (1024 + h * 256 + vc * 128) if gla else h * 128
                    P.dma("sp", actA[r0:r0 + 128, tt * 512:(tt + 1) * 512], sg[:], chan=f"st{st['stg'] % 4}")
                    st["stg"] += 1

            for r in range(-2, NU):
                A = P_steps(r + 2) if 0 <= r + 2 < NU else []
                B = E_steps(r + 1) if 0 <= r + 1 < NU else []
                C = R_steps(r) if 0 <= r < NU else []
                for k in range(max(len(A), len(B), len(C), 1)):
                    if k == 0:
                        for a in A[0:2]:
                            a()
                        if 0 <= r - 1 < NU:
                            F_stage(r - 1)
                    elif k >= 2 and k < len(A):
                        A[k]()
                    if k < len(C):
                        C[k]()
                    if k < len(B):
                        B[k]()
            F_stage(NU - 1)

    def xa_phase(l):
        W2d = xa_w_q[l]
        with Phase(P, f"xa{l}") as ph:
            actT = ph.sb("actT", [128, 16, T], BF16)
            wb = [ph.sb(f"wb{k}", [128, 16, 512], BF16) for k in range(2)]
            ws = WStream(wb, [(W2d, [(0, h_ * 512, 512)], 16) for h_ in range(4)])
            ws._issue(1)
            norm_T(f"xan{l}", lambda i: xs[i * 128:(i + 1) * 128, :], norm_xattn[l], actT, 16)
            kT = ph.sb("kT", [128, 16, MEM], BF16)
            vv = ph.sb("vv", [128, 2, D], BF16)
            qT = [ph.sb(f"qT{k}", [128, 4, T], BF16) for k in range(2)]
            eb_ = [ph.sb(f"e{k}", [128, 2, 512], BF16) for k in range(2)]
            lnd = [ph.sb(f"lnd{k}", [128, 512], F32) for k in range(2)]
            rinv = [ph.sb(f"rinv{k}", [128, 512], F32) for k in range(2)]
            stg = [ph.sb(f"stg{k}", [128, 512], BF16) for k in range(4)]
            pq = [ph.ps(f"pq{k}", [128, 512], F32) for k in range(2)]
            psS = [ph.ps(f"pss{k}", [128, 512], F32) for k in range(2)]
            pden = [ph.ps(f"pden{k}", [128, 512], F32) for k in range(2)]
            pov = [ph.ps(f"pov{k}", [128, 512], F32) for k in range(2)]
            P.dma("sp", kT[:], xaK[l].rearrange("(c p) m -> p c m", p=128), chan="kT")
            P.dma("sp", vv[:], xaV[l].rearrange("(b p) d -> p b d", p=128), chan="vv")
            cnt = dict(pq=0, ov=0)

            def projgroup(h, c, tt):
                w = ws.get(h)
                p = pq[cnt["pq"] % 2]
                cnt["pq"] += 1
                for kc in range(16):
                    P.mm(p[:], w[:, kc, c * 128:(c + 1) * 128], actT[:, kc, tt * 512:(tt + 1) * 512],
                         start=kc == 0, stop=kc == 15)
                P.act(qT[h % 2][:, c, tt * 512:(tt + 1) * 512], p[:], AF.Copy, scale=512 ** -0.5)

            def S1(u):
                h, tt = divmod(u, 4)
                q, e = qT[h % 2], eb_[u % 2]
                for mb in range(2):
                    p = psS[mb]
                    for c in range(4):
                        P.mm(p[:], kT[:, h * 4 + c, mb * 128:(mb + 1) * 128], q[:, c, tt * 512:(tt + 1) * 512],
                             start=c == 0, stop=c == 3)
                    P.act(e[:, mb, :], p[:], AF.Exp)

            def S2(u):
                e, pd = eb_[u % 2], pden[u % 2]
                for mb in range(2):
                    P.mm(pd[:], ones_b[:], e[:, mb, :], start=mb == 0, stop=mb == 1)
                P.act(lnd[u % 2][:], pd[:], AF.Ln)
                P.act(rinv[u % 2][:], lnd[u % 2][:], AF.Exp, scale=-1.0)

            def S3(u):
                h, tt = divmod(u, 4)
                e, ri = eb_[u % 2], rinv[u % 2]
                for c in range(4):
                    po_ = pov[cnt["ov"] % 2]
                    s = stg[cnt["ov"] % 4]
                    for mb in range(2):
                        P.mm(po_[:], vv[:, mb, h * 512 + c * 128: h * 512 + (c + 1) * 128], e[:, mb, :],
                             start=mb == 0, stop=mb == 1)
                    P.tt(s[:], po_[:], ri[:], ALU.mult)
                    r0 = (h * 4 + c) * 128
                    P.dma("sp", actA[r0:r0 + 128, tt * 512:(tt + 1) * 512], s[:], chan=f"st{cnt['ov'] % 4}")
                    cnt["ov"] += 1

            for c in range(4):
                for tt in range(4):
                    projgroup(0, c, tt)
            S1(0)
            for u in range(16):
                h, tt = divmod(u, 4)
                if h + 1 < 4:
                    for c in range(4):
                        projgroup(h + 1, c, tt)
                if u + 1 < 16:
                    S1(u + 1)
                S2(u)
                S3(u)

    def ffn_phase(l):
        W2d = ffn_w_in[l]
        with Phase(P, f"ff{l}") as ph:
            actT = ph.sb("actT", [128, 16, T], BF16)
            wb = [ph.sb(f"wb{k}", [128, 16, 512], BF16) for k in range(2)]
            ws = WStream(wb, [(W2d, [(0, wt_ * 256, 256), (256, DFF + wt_ * 256, 256)], 16) for wt_ in range(22)])
            ws._issue(1)
            norm_T(f"ffn{l}", lambda i: xs[i * 128:(i + 1) * 128, :], norm_ffn[l], actT, 16)
            gsb_ = [ph.sb(f"g{k}", [128, T + 2], F32) for k in range(2)]
            c1 = ph.sb("c1", [128, T], F32)
            c2 = ph.sb("c2", [128, T], F32)
            ab = [ph.sb(f"ab{k}", [128, T], BF16) for k in range(2)]
            PG = ph.ps("PG", [128, T], F32)
            PU = ph.ps("PU", [128, T], F32)
            for k in range(2):
                P.memset(gsb_[k][:, 0:2], 0.0)
            for wt in range(22):
                w = ws.get(wt)
                for cc in range(2):
                    j = wt * 2 + cc
                    g = gsb_[j % 2]
                    a = ab[j % 2]
                    for kc in range(16):
                        for tt in range(4):
                            P.mm(PG[:, tt * 512:(tt + 1) * 512], w[:, kc, 256 + cc * 128: 256 + (cc + 1) * 128],
                                 actT[:, kc, tt * 512:(tt + 1) * 512], start=kc == 0, stop=kc == 15)
                    P.copy(g[:, 2:T + 2], PG[:], eng="act")
                    for kc in range(16):
                        for tt in range(4):
                            P.mm(PU[:, tt * 512:(tt + 1) * 512], w[:, kc, cc * 128:(cc + 1) * 128],
                                 actT[:, kc, tt * 512:(tt + 1) * 512], start=kc == 0, stop=kc == 15)
                    P.ts(c1[:], g[:, 2:T + 2], cw[:, l, 2, j:j + 1], cb[:, l, j:j + 1], ALU.mult, ALU.add)
                    P.stt(c2[:], g[:, 1:T + 1], cw[:, l, 1, j:j + 1], c1[:], ALU.mult, ALU.add)
                    P.stt(c1[:], g[:, 0:T], cw[:, l, 0, j:j + 1], c2[:], ALU.mult, ALU.add)
                    P.act(c2[:], c1[:], AF.Silu)
                    P.tt(a[:], c2[:], PU[:], ALU.mult)
                    P.dma("sp", ffa[j * 128:(j + 1) * 128, :], a[:], chan=f"st{j % 2}")

    def sbqkv_phase():
        W2d = sb_w_qkv[0]
        with Phase(P, "sbp") as ph:
            actT = ph.sb("actT", [128, 16, T], BF16)
            wb = [ph.sb(f"wb{k}", [128, 16, 512], BF16) for k in range(2)]
            ws = WStream(wb, [(W2d, [(0, which_ * D + ft_ * 512, 512)], 16) for which_ in range(3) for ft_ in range(4)])
            ws._issue(1)
            norm_T("sbn", lambda i: xs[i * 128:(i + 1) * 128, :], norm_mix[1], actT, 16)
            stg = [ph.sb(f"stg{k}", [128, T], BF16) for k in range(2)]
            stv = [ph.sb(f"stv{k}", [128, 512], BF16) for k in range(4)]
            pq = [ph.ps(f"pq{k}", [128, T], F32) for k in range(2)]
            wi = 0
            u = 0
            for which in range(2):
                dst = sbq if which == 0 else sbk
                for ft in range(4):
                    w = ws.get(wi)
                    wi += 1
                    for c in range(4):
                        p = pq[u % 2]
                        s = stg[u % 2]
                        u += 1
                        for kc in range(16):
                            for tt in range(4):
                                P.mm(p[:, tt * 512:(tt + 1) * 512], w[:, kc, c * 128:(c + 1) * 128],
                                     actT[:, kc, tt * 512:(tt + 1) * 512], start=kc == 0, stop=kc == 15)
                        if which == 0:
                            P.act(s[:], p[:], AF.Copy, scale=128 ** -0.5)
                        else:
                            P.copy(s[:], p[:], eng="dve")
                        r0 = (ft * 4 + c) * 128
                        P.dma("sp", dst[r0:r0 + 128, :], s[:], chan=f"st{(u - 1) % 2}")
            v = 0
            for ft in range(4):
                w = ws.get(wi)
                wi += 1
                for i in range(16):
                    p = pq[v % 2]
                    s = stv[v % 4]
                    v += 1
                    for kc in range(16):
                        P.mm(p[:, 0:512], actT[:, kc, i * 128:(i + 1) * 128], w[:, kc, :], start=kc == 0, stop=kc == 15)
                    P.copy(s[:], p[:, 0:512], eng="act" if i % 2 else "dve")
                    P.dma("sp", sbv[i * 128:(i + 1) * 128, ft * 512:(ft + 1) * 512], s[:], chan=f"sv{(v - 1) % 4}")

    def sbatt_phase():
        with Phase(P, "sba") as ph:
            qh = [ph.sb(f"qh{k}", [128, T], BF16) for k in range(2)]
            kh = [ph.sb(f"kh{k}", [128, T], BF16) for k in range(2)]
            vh = [ph.sb(f"vh{k}", [128, 16, 128], BF16) for k in range(2)]
            NBUF = 3
            ez = [ph.sb(f"ez{k}", [128, 512], F32) for k in range(NBUF)]
            sp = [ph.sb(f"sp{k}", [128, 512], F32) for k in range(NBUF)]
            cum = [ph.sb(f"cum{k}", [128, 512], F32) for k in range(2)]
            att = [ph.sb(f"att{k}", [128, 512], BF16) for k in range(NBUF)]
            stg = [ph.sb(f"stg{k}", [128, T], BF16) for k in range(2)]
            pz = [ph.ps(f"pz{k}", [128, 512], F32) for k in range(3)]
            pl = pz
            po = [ph.ps(f"po{k}", [128, 512], F32) for k in range(2)]
            sbm_f = ph.sb("sbm_f", [128, 4, 512], F32)
            sbm_b = ph.sb("sbm_b", [128, 4, 512], BF16)
            for j in range(4):
                P.memset(sbm_f[:, j, :], 1.0, eng="pool")
                P.aselect(sbm_f[:, j, :], sbm_f[:, j, :], ALU.is_gt, 0.0, -128 * j, [[1, 512]], -1)
            P.copy(sbm_b[:], sbm_f[:], eng="pool")

            def load_head(h):
                P.dma("sp", qh[h % 2][:], sbq[h * 128:(h + 1) * 128, :], chan=f"ldq{h % 2}")
                P.dma("sp", kh[h % 2][:], sbk[h * 128:(h + 1) * 128, :], chan=f"ldk{h % 2}")
                P.dma("sp", vh[h % 2][:], sbv[:, h * 128:(h + 1) * 128].rearrange("(b p) d -> p b d", p=128),
                      chan=f"ldv{h % 2}", allow_slow_non_contiguous=True)

            its = []
            g = 0
            for h in range(16):
                for TT in range(4):
                    nb = 4 * TT + 4
                    for bi, B in enumerate(range(nb - 1, -1, -1)):
                        its.append(dict(h=h, TT=TT, B=B, bi=bi, g=g, last=(B == 0)))
                    g += 1
            n = len(its)

            def ctx(i):
                it = its[i]
                h, TT, B = it["h"], it["TT"], it["B"]
                q, k, v = qh[h % 2], kh[h % 2], vh[h % 2]
                return (it, q[:, TT * 512:(TT + 1) * 512], k[:, B * 128:(B + 1) * 128], v[:, B, :],
                        pz[i % 3], pl[i % 3], ez[i % NBUF], sp[i % NBUF], att[i % NBUF],
                        cum[it["g"] % 2], po[it["g"] % 2], B - 4 * TT)

            def stA(i):
                it, qt, kb, vb, z, lp, e, s_, a, cm, o, jd = ctx(i)
                if it["TT"] == 0 and it["bi"] == 0:
                    if it["h"] == 0:
                        load_head(0)
                    if it["h"] + 1 < 16:
                        load_head(it["h"] + 1)
                c0 = 128 * max(jd, 0)
                P.mm(z[:, c0:], kb, qt[:, c0:], start=True, stop=False, skip_group_check=True)
                P.act(e[:, c0:], z[:, c0:], AF.Exp)
                P.act(s_[:, c0:], e[:, c0:], AF.Ln, bias=1.0)
                if jd >= 0:
                    P.tt(s_[:, c0:c0 + 128], s_[:, c0:c0 + 128], sbm_f[:, 0, 0:128], ALU.mult, eng="pool")

            def stC(i):
                it, qt, kb, vb, z, lp, e, s_, a, cm, o, jd = ctx(i)
                bi, B = it["bi"], it["B"]
                lp = z
                c0 = 128 * max(jd, 0)
                P.mm(lp[:, c0:], negtri[:], s_[:, c0:], start=False, stop=(bi == 0), skip_group_check=True)
                if bi > 0:
                    P.mm(lp[:, c0:], negones[:], cm[:, c0:], start=False, stop=True, skip_group_check=True)
                if bi == 0:
                    if c0 > 0:
                        P.memset(cm[:, 0:c0], 0.0)
                    P.copy(cm[:, c0:], s_[:, c0:], eng="dve")
                elif B > 0:
                    P.tt(cm[:, c0:], cm[:, c0:], s_[:, c0:], ALU.add)
                P.act(a[:, c0:], lp[:, c0:], AF.Exp)
                if jd >= 0:
                    P.tt(a[:, c0:c0 + 128], a[:, c0:c0 + 128], sbm_b[:, 0, 0:128], ALU.mult, eng="pool")

            def stF(i):
                it, qt, kb, vb, z, lp, e, s_, a, cm, o, jd = ctx(i)
                h, TT = it["h"], it["TT"]
                c0 = 128 * max(jd, 0)
                P.mm(o[:, c0:], vb, a[:, c0:], start=(it["bi"] == 0), stop=it["last"], skip_group_check=True)
                if it["last"]:
                    s = stg[h % 2]
                    P.copy(s[:, TT * 512:(TT + 1) * 512], o[:], eng="dve")
                    if TT == 3:
                        P.dma("sp", actA[h * 128:(h + 1) * 128, :], s[:], chan=f"st{h % 2}")

            for r in range(n + 2):
                if r < n:
                    stA(r)
                if 0 <= r - 1 < n:
                    stC(r - 1)
                if 0 <= r - 2 < n:
                    stF(r - 2)

    def final_phase():
        with Phase(P, "fin") as ph:
            gbc = ph.sb("gbc", [128, D], F32)
            P.dma("sp", gbc[:], final_norm.partition_broadcast(128), chan="gbc")
            xt = [ph.sb(f"xt{k}", [128, D], F32) for k in range(3)]
            junk = ph.sb("junk", [128, D], BF16)
            ss = [ph.sb(f"ss{k}", [128, 1], F32) for k in range(3)]
            for i in range(16):
                b = i % 3
                P.dma("sp", xt[b][:], xs[i * 128:(i + 1) * 128, :], chan=f"fx{b}")
                P.act(junk[:], xt[b][:], AF.Square, accum_out=ss[b][:])
                P.act(ss[b][:], ss[b][:], AF.Sqrt, bias=1e-6, scale=1.0 / D)
                P.recip(ss[b][:], ss[b][:])
                P.stt(xt[b][:], xt[b][:], ss[b][:], gbc[:], ALU.mult, ALU.mult)
                P.dma("sp", out[i * 128:(i + 1) * 128, :], xt[b][:], chan=f"fx{b}")

    def run():
        mem_phase()
        if stopped("mem"):
            return
        ab_phase(x_in)
        if stopped("ab"):
            return
        outproj("abo", actA, D, ab_w_out[0], x_in, xs)
        if stopped("abo"):
            return
        for l in range(2):
            if l == 1:
                sbqkv_phase()
                if stopped("sbp"):
                    return
                sbatt_phase()
                if stopped("sba"):
                    return
                outproj("sbo", actA, D, sb_w_out[0], xs, xs)
                if stopped("sbo"):
                    return
            xa_phase(l)
            if stopped(f"xa{l}"):
                return
            outproj(f"xao{l}", actA, D, xa_w_o[l], xs, xs)
            if stopped(f"xao{l}"):
                return
            ffn_phase(l)
            if stopped(f"ff{l}"):
                return
            outproj(f"ffo{l}", ffa, DFF, ffn_w_out[l], xs, xs)
            if stopped(f"ffo{l}"):
                return
        final_phase()

    run()
    P.finish(top)
    top.close()
    return nc, P


INPUT_NAMES = ["mem_norm", "norm_mix", "norm_xattn", "norm_ffn", "ab_w_in", "hgrn_lb_logits", "hgrn_norm",
               "gla_w_gk", "gla_b_gk", "gla_norm", "ab_w_out", "sb_w_qkv", "sb_w_out", "xa_w_q", "xa_w_kv",
               "xa_w_o", "ffn_w_in", "ffn_conv_w", "ffn_conv_b", "ffn_w_out", "final_norm"]


def kernel(**inputs):
    n = 8
    nc, _ = build()
    shared = {k: np.ascontiguousarray(np.asarray(inputs[k], dtype=np.float32)) for k in INPUT_NAMES}
    x = np.asarray(inputs["x"], dtype=np.float32)
    mem = np.asarray(inputs["mem"], dtype=np.float32)
    in_maps = []
    for b in range(n):
        m = dict(shared)
        m["x"] = np.ascontiguousarray(x[b])
        m["mem"] = np.ascontiguousarray(mem[b])
        in_maps.append(m)
    res = run_bass_kernel_spmd(nc, in_maps, core_ids=list(range(n)))
    return np.stack([np.asarray(r["out"], dtype=np.float32) for r in res.results], axis=0)
```

```python
import math
from contextlib import ExitStack

import numpy as np
import concourse.bass as bass
import concourse.mybir as mybir
from concourse.bass_utils import run_bass_kernel_spmd

F32 = mybir.dt.float32
BF16 = mybir.dt.bfloat16
AF = mybir.ActivationFunctionType
ALU = mybir.AluOpType

T = 2048
D = 2048
DFF = 5632
MEM = 256
ENGS = ("pe", "act", "dve", "pool", "sp")


class KAP:
    def __init__(self, ap, keys):
        self.ap, self.keys = ap, list(keys)


def _ap(a):
    return a.ap if isinstance(a, KAP) else a


def _keys(a):
    if a is None or isinstance(a, (int, float)):
        return []
    if isinstance(a, KAP):
        return list(a.keys)
    if "DRAM" in str(a.space):
        return []
    return [a.name]


class Prog:
    def __init__(self, nc):
        self.nc = nc
        self.ops = []
        self.eng = {"pe": nc.tensor, "act": nc.scalar, "dve": nc.vector, "pool": nc.gpsimd, "sp": nc.sync}

    def op(self, eng, fn, reads, writes, dma=None):
        self.ops.append((eng, fn, tuple(reads), tuple(writes), dma))

    def barrier(self):
        self.ops.append(("barrier",))

    def mm(self, out, lhsT, rhs, start=True, stop=True, **kw):
        o, l, r = _ap(out), _ap(lhsT), _ap(rhs)
        self.op("pe", lambda E: E.matmul(o, l, r, start=start, stop=stop, **kw), _keys(lhsT) + _keys(rhs), _keys(out))

    def transpose(self, out, in_, ident):
        o, i, d = _ap(out), _ap(in_), _ap(ident)
        self.op("pe", lambda E: E.transpose(o, i, d), _keys(in_) + _keys(ident), _keys(out))

    def act(self, out, in_, func, bias=None, scale=1.0, accum_out=None, eng="act"):
        o, i = _ap(out), _ap(in_)
        b = _ap(bias) if bias is not None else None
        s = _ap(scale)
        a = _ap(accum_out) if accum_out is not None else None
        kw = {}
        if b is not None:
            kw["bias"] = b
        if a is not None:
            kw["accum_out"] = a
        self.op(eng, lambda E: E.activation(out=o, in_=i, func=func, scale=s, **kw),
                _keys(in_) + _keys(bias) + _keys(scale), _keys(out) + _keys(accum_out))

    def tt(self, out, in0, in1, op, eng="dve"):
        o, a, b = _ap(out), _ap(in0), _ap(in1)
        self.op(eng, lambda E: E.tensor_tensor(o, a, b, op), _keys(in0) + _keys(in1), _keys(out))

    def ts(self, out, in0, s1, s2, op0, op1=None, eng="dve"):
        o, a, x1, x2 = _ap(out), _ap(in0), _ap(s1), _ap(s2)
        if op1 is None:
            fn = lambda E: E.tensor_scalar(o, a, x1, None, op0)
        else:
            fn = lambda E: E.tensor_scalar(o, a, x1, x2, op0, op1)
        self.op(eng, fn, _keys(in0) + _keys(s1) + _keys(s2), _keys(out))

    def stt(self, out, in0, scalar, in1, op0, op1, eng="dve"):
        o, a, s, b = _ap(out), _ap(in0), _ap(scalar), _ap(in1)
        self.op(eng, lambda E: E.scalar_tensor_tensor(o, a, s, b, op0, op1),
                _keys(in0) + _keys(scalar) + _keys(in1), _keys(out))

    def copy(self, out, in_, eng="dve"):
        o, i = _ap(out), _ap(in_)
        if eng == "act":
            self.op(eng, lambda E: E.activation(out=o, in_=i, func=AF.Copy), _keys(in_), _keys(out))
        else:
            self.op(eng, lambda E: E.tensor_copy(o, i), _keys(in_), _keys(out))

    def scan(self, out, d0, d1, init, op0, op1):
        o, a, b = _ap(out), _ap(d0), _ap(d1)
        self.op("dve", lambda E: E.tensor_tensor_scan(o, a, b, init, op0, op1), _keys(d0) + _keys(d1), _keys(out))

    def recip(self, out, in_):
        o, i = _ap(out), _ap(in_)
        self.op("dve", lambda E: E.reciprocal(o, i), _keys(in_), _keys(out))

    def memset(self, out, val, eng="dve"):
        o = _ap(out)
        self.op(eng, lambda E: E.memset(o, val), [], _keys(out))

    def aselect(self, out, in_, cmp, fill, base, pattern, cm):
        o, i = _ap(out), _ap(in_)
        self.op("pool", lambda E: E.affine_select(out=o, in_=i, compare_op=cmp, fill=fill, base=base,
                                                    pattern=pattern, channel_multiplier=cm), _keys(in_), _keys(out))

    def dma(self, eng, out, in_, chan, rk=(), wk=(), **kw):
        o, i = _ap(out), _ap(in_)
        self.op(eng, lambda E: E.dma_start(out=o, in_=i, **kw), _keys(in_) + list(rk), _keys(out) + list(wk), dma=chan)

    def finish(self, stack):
        nc = self.nc
        ops = self.ops
        n = len(ops)
        last_w = {}
        readers = {}
        deps = [None] * n
        need_inc = [False] * n
        last_on = {}
        pend = {}
        for i, o in enumerate(ops):
            if o[0] == "barrier":
                allidx = list(last_on.values())
                for e in ENGS:
                    pend[e] = list(allidx)
                continue
            eng, fn, R, W, dma = o
            d = {}
            for k in R:
                j = last_w.get(k)
                if j is not None:
                    d[j] = "raw"
            for k in W:
                j = last_w.get(k)
                if j is not None and j not in d:
                    d[j] = "waw"
                for j in readers.get(k, {}).values():
                    if j not in d:
                        d[j] = "war"
            if eng in pend:
                for j in pend.pop(eng):
                    d[j] = "raw"
            keep = []
            for j, kind in d.items():
                if j == i:
                    continue
                oj = ops[j]
                if oj[4] is None and oj[0] == eng and dma is None:
                    if eng == "pe":
                        continue
                keep.append(j)
            keep.sort()
            deps[i] = keep
            for j in keep:
                need_inc[j] = True
            src = dma if dma is not None else eng
            for k in W:
                last_w[k] = i
                readers[k] = {}
            for k in R:
                readers.setdefault(k, {})[src] = i
            last_on[src] = i
        sem_eng = {e: stack.enter_context(nc.semaphore("S_" + e)) for e in ("pe", "act", "dve", "pool")}
        cnt_eng = {e: 0 for e in sem_eng}
        chan_sem = {}
        chan_cnt = {}
        token = [None] * n
        waited = {e: {} for e in ENGS}
        self.trace = {e: [] for e in ENGS}
        for i, o in enumerate(ops):
            if o[0] == "barrier":
                continue
            eng, fn, R, W, dma = o
            E = self.eng[eng]
            myw = []
            for j in deps[i]:
                oj = ops[j]
                if oj[4] is not None:
                    sem = chan_sem[oj[4]]
                    val = chan_cnt[oj[4]]
                else:
                    sem, val = token[j]
                if waited[eng].get(sem.name, 0) >= val:
                    continue
                E.wait_ge(sem, val)
                waited[eng][sem.name] = val
                myw.append((sem.name, val))
            ins = fn(E)
            if dma is not None:
                if dma not in chan_sem:
                    chan_sem[dma] = stack.enter_context(nc.semaphore("C_" + dma))
                    chan_cnt[dma] = 0
                chan_cnt[dma] += 16
                ins.then_inc(chan_sem[dma], 16)
                self.trace[eng].append((myw, (chan_sem[dma].name, 16), i))
            elif need_inc[i]:
                cnt_eng[eng] += 1
                ins.then_inc(sem_eng[eng], 1)
                token[i] = (sem_eng[eng], cnt_eng[eng])
                self.trace[eng].append((myw, (sem_eng[eng].name, 1), i))
            else:
                self.trace[eng].append((myw, None, i))
        E = self.eng["sp"]
        for c, sem in chan_sem.items():
            if waited["sp"].get(sem.name, 0) < chan_cnt[c]:
                E.wait_ge(sem, chan_cnt[c])
        self.stats = dict(n_ops=n, cnt_eng=dict(cnt_eng), n_chan=len(chan_sem))

    def simulate(self):
        sems = {}
        pos = {e: 0 for e in ENGS}
        progress = True
        while progress:
            progress = False
            for e in ENGS:
                tr = self.trace[e]
                while pos[e] < len(tr):
                    waits, inc, i = tr[pos[e]]
                    if any(sems.get(sn, 0) < v for sn, v in waits):
                        break
                    if inc is not None:
                        sems[inc[0]] = sems.get(inc[0], 0) + inc[1]
                    pos[e] += 1
                    progress = True
        stuck = {e: (pos[e], len(self.trace[e])) for e in ENGS if pos[e] < len(self.trace[e])}
        info = {}
        for e in stuck:
            waits, inc, i = self.trace[e][pos[e]]
            info[e] = dict(op=i, waits=[(sn, v, sems.get(sn, 0)) for sn, v in waits])
        return stuck, info


class Phase:
    _id = 0

    def __init__(self, P, name):
        self.P, self.nc = P, P.nc
        Phase._id += 1
        self.pfx = f"{name}{Phase._id}_"
        self.stack = ExitStack()

    def __enter__(self):
        self.stack.__enter__()
        return self

    def __exit__(self, *a):
        self.P.barrier()
        return self.stack.__exit__(*a)

    def sb(self, name, shape, dt):
        return self.stack.enter_context(self.nc.sbuf_tensor(self.pfx + name, list(shape), dt))

    def ps(self, name, shape, dt=F32):
        return self.stack.enter_context(self.nc.psum_tensor(self.pfx + name, list(shape), dt))


def wview(W2d, c0, n):
    return W2d[:, c0:c0 + n].rearrange("(kc p) f -> p kc f", p=128)


def build(dbg=None):
    dbg = dbg or {}
    stop_after = dbg.get("stop_after")
    dump = set(dbg.get("dump", ()))
    nc = bass.Bass("TRN2", target_bir_lowering=False)
    P = Prog(nc)
    top = ExitStack()

    def din(name, shape):
        return nc.dram_tensor(name, list(shape), F32, kind="ExternalInput").ap()

    x_in = din("x", [T, D])
    mem_in = din("mem", [MEM, D])
    mem_norm = din("mem_norm", [D])
    norm_mix = din("norm_mix", [2, D])
    norm_xattn = din("norm_xattn", [2, D])
    norm_ffn = din("norm_ffn", [2, D])
    ab_w_in = din("ab_w_in", [1, D, 7184])
    hgrn_lb_logits = din("hgrn_lb_logits", [2, 1024])
    hgrn_norm = din("hgrn_norm", [1, 128])
    gla_w_gk = din("gla_w_gk", [1, 16, 512])
    gla_b_gk = din("gla_b_gk", [1, 512])
    gla_norm = din("gla_norm", [1, 256])
    ab_w_out = din("ab_w_out", [1, D, D])
    sb_w_qkv = din("sb_w_qkv", [1, D, 3 * D])
    sb_w_out = din("sb_w_out", [1, D, D])
    xa_w_q = din("xa_w_q", [2, D, D])
    xa_w_kv = din("xa_w_kv", [2, D, 2 * D])
    xa_w_o = din("xa_w_o", [2, D, D])
    ffn_w_in = din("ffn_w_in", [2, D, 2 * DFF])
    ffn_conv_w = din("ffn_conv_w", [2, 3, DFF])
    ffn_conv_b = din("ffn_conv_b", [2, DFF])
    ffn_w_out = din("ffn_w_out", [2, DFF, D])
    final_norm = din("final_norm", [D])
    out = nc.dram_tensor("out", [T, D], F32, kind="ExternalOutput").ap()

    def scr(name, shape, dt):
        kind = "ExternalOutput" if name in dump else "Internal"
        return nc.dram_tensor(name, list(shape), dt, kind=kind).ap()

    xs = scr("xs", [T, D], F32)
    actA = scr("actA", [D, T], BF16)
    ffa = scr("ffa", [DFF, T], BF16)
    sbq = scr("sbq", [D, T], BF16)
    sbk = scr("sbk", [D, T], BF16)
    sbv = scr("sbv", [T, D], BF16)
    xaK = [scr(f"xaK{l}", [D, MEM], BF16) for l in range(2)]
    xaV = [scr(f"xaV{l}", [MEM, D], BF16) for l in range(2)]

    def gsb(name, shape, dt):
        return top.enter_context(nc.sbuf_tensor(name, list(shape), dt))

    ident = gsb("ident", [128, 128], BF16)
    ones_f = gsb("ones_f", [128, 128], F32)
    ones_b = gsb("ones_b", [128, 128], BF16)
    negtri = gsb("negtri", [128, 128], F32)
    negones = gsb("negones", [128, 128], F32)
    pmask = gsb("pmask", [128, 128], F32)
    rmask = gsb("rmask", [128, 512], F32)
    lgt = gsb("lgt", [128, 2, 8], F32)
    lb = gsb("lb", [128, 8], F32)
    oml = gsb("oml", [128, 8], F32)
    lnoml = gsb("lnoml", [128, 8], F32)
    hgain = gsb("hgain", [128, 1], F32)
    ggain = gsb("ggain", [128, 2], F32)
    negb = gsb("negb", [128, 4], F32)
    wgk = gsb("wgk", [16, 512], F32)
    cw = gsb("cw", [128, 2, 3, 44], F32)
    cb = gsb("cb", [128, 2, 44], F32)

    P.memset(ident[:], 0.0, eng="pool")
    P.aselect(ident[:], ident[:], ALU.not_equal, 1.0, 0, [[-1, 128]], 1)
    P.memset(ones_f[:], 1.0, eng="pool")
    P.memset(ones_b[:], 1.0, eng="pool")
    P.memset(negones[:], -1.0, eng="pool")
    P.memset(negtri[:], -1.0, eng="pool")
    P.aselect(negtri[:], negtri[:], ALU.is_ge, 0.0, 0, [[-1, 128]], 1)
    P.memset(pmask[:], 1.0, eng="pool")
    P.aselect(pmask[:], pmask[:], ALU.is_ge, 0.0, 0, [[1, 128]], -1)
    P.memset(pmask[0:64, 64:128], 0.0, eng="pool")
    P.memset(rmask[:], 1.0, eng="pool")
    P.memset(rmask[:].rearrange("p (c t) -> p c t", t=64)[:, :, 0:1], 0.0, eng="pool")
    NCD = dict(allow_slow_non_contiguous=True)
    P.dma("sp", lgt[:], hgrn_lb_logits.rearrange("r (h p) -> p r h", p=128), chan="c0", **NCD)
    P.dma("sp", hgain[:], hgrn_norm.rearrange("o p -> p o"), chan="c0", **NCD)
    P.dma("sp", ggain[:], gla_norm[0].rearrange("(c p) -> p c", p=128), chan="c0", **NCD)
    P.dma("sp", negb[:], gla_b_gk[0].rearrange("(h p) -> p h", p=128), chan="c0", **NCD)
    P.dma("sp", wgk[:], gla_w_gk[0], chan="c0")
    for l in range(2):
        for j in range(3):
            P.dma("sp", cw[:, l, j, :], ffn_conv_w[l, j].rearrange("(c p) -> p c", p=128), chan="c0", **NCD)
        P.dma("sp", cb[:, l, :], ffn_conv_b[l].rearrange("(c p) -> p c", p=128), chan="c0", **NCD)
    P.tt(lb[:], lgt[:, 0, :], lgt[:, 1, :], ALU.subtract)
    P.act(lb[:], lb[:], AF.Sigmoid)
    P.ts(oml[:], lb[:], -1.0, 1.0, ALU.mult, ALU.add)
    P.act(lnoml[:], oml[:], AF.Ln)
    P.ts(negb[:], negb[:], -1.0, None, ALU.mult)
    P.barrier()

    def stopped(name):
        return stop_after is not None and stop_after == name

    def norm_T(name, x_fn, gain_ap, actT, ntiles):
        with Phase(P, name) as ph:
            gbc = ph.sb("gbc", [128, D], F32)
            P.dma("sp", gbc[:], gain_ap.partition_broadcast(128), chan="gbc")
            NX = 4
            xt = [ph.sb(f"xt{k}", [128, D], F32) for k in range(NX)]
            hb = [ph.sb(f"hb{k}", [128, D], BF16) for k in range(NX)]
            junk = ph.sb("junk", [128, D], BF16)
            ss = [ph.sb(f"ss{k}", [128, 1], F32) for k in range(NX)]
            rs = [ph.sb(f"rs{k}", [128, 1], F32) for k in range(NX)]
            tp = [ph.ps(f"tp{k}", [128, 8, 128], BF16) for k in range(4)]
            def st1(i):
                b = i % NX
                P.dma("sp", xt[b][:], x_fn(i), chan=f"xt{b}")
                P.act(junk[:], xt[b][:], AF.Square, accum_out=ss[b][:])
                P.act(rs[b][:], ss[b][:], AF.Sqrt, bias=1e-6, scale=1.0 / D)
                P.recip(rs[b][:], rs[b][:])
                P.stt(hb[b][:], xt[b][:], rs[b][:], gbc[:], ALU.mult, ALU.mult)

            def st2(i):
                b = i % NX
                for half in range(2):
                    t = tp[(i * 2 + half) % 4]
                    for c in range(8):
                        P.transpose(t[:, c, :], hb[b][:, (half * 8 + c) * 128:(half * 8 + c + 1) * 128], ident[:])
                    P.copy(actT[:, half * 8:(half + 1) * 8, i * 128:(i + 1) * 128], t[:],
                           eng="act" if half == 0 else "dve")

            st1(0)
            for i in range(ntiles):
                if i + 1 < ntiles:
                    st1(i + 1)
                st2(i)

    def load_w(wb, W2d, specs, KC, slot=0):
        for (d0, s0, n) in specs:
            P.dma("pool", wb[:, 0:KC, d0:d0 + n], wview(W2d, s0, n), chan=f"w{slot}")

    class WStream:
        def __init__(self, wb, loads):
            self.wb, self.loads, self.issued = wb, loads, 0

        def _issue(self, k):
            while self.issued <= k and self.issued < len(self.loads):
                W2d, specs, KC = self.loads[self.issued]
                load_w(self.wb[self.issued % 2], W2d, specs, KC, slot=self.issued % 2)
                self.issued += 1

        def get(self, k):
            self._issue(k + 1)
            return self.wb[k % 2]

    def load_actT(actT, src, KC, c0=0, ncols=T):
        for g in range(0, KC, 4):
            ge = min(KC, g + 4)
            P.dma("sp", KAP(actT[:, g:ge, 0:ncols], [f"{actT.name}.g{g // 4}"]),
                  src[g * 128:ge * 128, c0:c0 + ncols].rearrange("(kc p) t -> p kc t", p=128), chan=f"actT{(g // 4) % 4}")

    def outproj(name, src, K, W2d, x_src, x_dst):
        KC = K // 128
        G = 1 if KC <= 16 else 2
        FT = 512
        NF = D // FT
        TG = T // G
        with Phase(P, name) as ph:
            actT = ph.sb("actT", [128, KC, TG], BF16)
            wb = [ph.sb(f"wb{k}", [128, KC, FT], BF16) for k in range(2)]
            xin = [ph.sb(f"xin{k}", [128, FT], F32) for k in range(4)]
            ps = [ph.ps(f"ps{k}", [128, FT], F32) for k in range(4)]
            ws = WStream(wb, [(W2d, [(0, f * FT, FT)], KC) for g in range(G) for f in range(NF)])
            units = [(g, f, i) for g in range(G) for f in range(NF) for i in range(TG // 128)]

            def ldx(u):
                g, f, i = units[u]
                r0 = g * TG + i * 128
                P.dma("sp", xin[u % 4][:], x_src[r0:r0 + 128, f * FT:(f + 1) * FT], chan=f"xin{u % 4}")

            PF = int(dbg.get("pf", 2))
            for u in range(min(PF, len(units))):
                ldx(u)
            for u, (g, f, i) in enumerate(units):
                if f == 0 and i == 0:
                    load_actT(actT, src, KC, g * TG, TG)
                w = ws.get(g * NF + f)
                r0 = g * TG + i * 128
                p = ps[u % 4]
                xi = xin[u % 4]
                if u + PF < len(units):
                    ldx(u + PF)
                for kc in range(KC):
                    P.mm(p[:], KAP(actT[:, kc, i * 128:(i + 1) * 128], [f"{actT.name}.g{kc // 4}"]), w[:, kc, :],
                         start=kc == 0, stop=kc == KC - 1)
                P.tt(xi[:], xi[:], p[:], ALU.add)
                P.dma("act", x_dst[r0:r0 + 128, f * FT:(f + 1) * FT], xi[:], chan=f"xin{u % 4}")

    def mem_phase():
        with Phase(P, "mem") as ph:
            memT = ph.sb("memT", [128, 16, MEM], BF16)
            norm_T("memn", lambda i: mem_in[i * 128:(i + 1) * 128, :], mem_norm, memT, 2)
            wb = [ph.sb(f"wb{k}", [128, 16, 512], BF16) for k in range(2)]
            st = [ph.sb(f"st{k}", [128, 512], BF16) for k in range(4)]
            ps = [ph.ps(f"ps{k}", [128, 512], F32) for k in range(4)]
            u = 0
            wi = 0
            ws = WStream(wb, [(xa_w_kv[l], [(0, half * D + ft * 512, 512)], 16)
                              for l in range(2) for half in range(2) for ft in range(4)])
            for l in range(2):
                W2d = xa_w_kv[l]
                for ft in range(4):
                    w = ws.get(wi)
                    wi += 1
                    for c in range(4):
                        p = ps[u % 4]
                        s = st[u % 4]
                        for kc in range(16):
                            P.mm(p[:, 0:MEM], w[:, kc, c * 128:(c + 1) * 128], memT[:, kc, :], start=kc == 0, stop=kc == 15)
                        P.copy(s[:, 0:MEM], p[:, 0:MEM], eng="act")
                        r0 = (ft * 4 + c) * 128
                        P.dma("sp", xaK[l][r0:r0 + 128, :], s[:, 0:MEM], chan=f"st{u % 4}")
                        u += 1
                for ft in range(4):
                    w = ws.get(wi)
                    wi += 1
                    for mb in range(2):
                        p = ps[u % 4]
                        s = st[u % 4]
                        for kc in range(16):
                            P.mm(p[:], memT[:, kc, mb * 128:(mb + 1) * 128], w[:, kc, :], start=kc == 0, stop=kc == 15)
                        P.copy(s[:], p[:], eng="dve")
                        P.dma("sp", xaV[l][mb * 128:(mb + 1) * 128, ft * 512:(ft + 1) * 512], s[:], chan=f"st{u % 4}")
                        u += 1

    def ab_phase(x_src):
        W2d = ab_w_in[0]
        with Phase(P, "ab") as ph:
            actT = ph.sb("actT", [128, 16, T], BF16)
            wb = [ph.sb(f"wb{k}", [128, 16, 768], BF16) for k in range(2)]
            wlow = ph.sb("wlow", [128, 16, 16], BF16)
            load_w(wlow, W2d, [(0, 7168, 16)], 16, slot=2)
            hl = []
            for hh_ in range(12):
                h_ = hh_ - 8 if hh_ >= 8 else hh_
                if hh_ >= 8:
                    hl.append((W2d, [(0, 4096 + h_ * 128, 128), (128, 4608 + h_ * 128, 128),
                                     (256, 5120 + h_ * 256, 256), (512, 6144 + h_ * 256, 256)], 16))
                else:
                    hl.append((W2d, [(0, h_ * 128, 128), (128, 1024 + h_ * 128, 128),
                                     (256, 2048 + h_ * 128, 128), (384, 3072 + h_ * 128, 128)], 16))
            ws = WStream(wb, hl)
            ws._issue(1)
            norm_T("abn", lambda i: x_src[i * 128:(i + 1) * 128, :], norm_mix[0], actT, 16)
            glow = ph.sb("glow", [16, T], F32)
            f32t = lambda nm: ph.sb(nm, [128, 512], F32)
            bft = lambda nm: ph.sb(nm, [128, 512], BF16)
            Qb = [f32t(f"Q{k}") for k in range(2)]
            Kb = [f32t(f"K{k}") for k in range(2)]
            LG = [f32t(f"LG{k}") for k in range(2)]
            GA = [[f32t(f"G{k}_{v}") for v in range(2)] for k in range(4)]
            VT = [ph.sb(f"vt{k}", [128, 4, 256], BF16) for k in range(3)]
            bb, d1, d2, exA, exB, ptmp, ptmp2 = [f32t(n_) for n_ in ("bb", "d1", "d2", "exA", "exB", "ptmp", "ptmp2")]
            qs, ks, kd = bft("qs"), bft("ks"), bft("kd")
            QD = [bft(f"qd{k}") for k in range(2)]
            KDT = [ph.sb(f"kdt{k}", [128, 4, 128], BF16) for k in range(2)]
            DEC = [ph.sb(f"dec{k}", [128, 8], F32) for k in range(2)]
            SCB = [ph.sb(f"scb{k}", [128, 4, 128], BF16) for k in range(2)]
            osb = [f32t("o0"), f32t("o1")]
            sqs = [f32t("sq0"), f32t("sq1")]
            rstd = f32t("rstd")
            S = ph.sb("S", [128, 256], F32)
            Sb = [ph.sb(f"Sb{k}", [128, 256], BF16) for k in range(2)]
            stg = [bft(f"stg{k}") for k in range(4)]
            PB = [ph.ps(f"pb{k}", [128, 512], F32) for k in range(3)]
            psc = ph.ps("psc", [128, 512], F32)
            po = [ph.ps(f"po{k}", [128, 512], F32) for k in range(2)]
            pds = ph.ps("pds", [128, 512], F32)
            ptr = ph.ps("ptr", [128, 4, 128], BF16)
            st = dict(pb=0, cur=0, stg=0)

            def nextpb():
                st["pb"] += 1
                return PB[st["pb"] % 3]

            for tt in range(4):
                p = nextpb()
                for kc in range(16):
                    P.mm(p[0:16, :], wlow[:, kc, :], actT[:, kc, tt * 512:(tt + 1) * 512], start=kc == 0, stop=kc == 15)
                P.copy(glow[:, tt * 512:(tt + 1) * 512], p[0:16, :], eng="act")
            NU = 48

            def info(u):
                hh, tt = divmod(u, 4)
                gla = hh >= 8
                h = hh - 8 if gla else hh
                return hh, tt, gla, h, (256 if gla else 128)

            def P_steps(u):
                hh, tt, gla, h, V = info(u)
                steps = []
                Q, K_, L = Qb[u % 2], Kb[u % 2], LG[u % 2]
                G = GA[u % 4]
                vt = VT[u % 3]
                tsl = slice(tt * 512, (tt + 1) * 512)

                def fgroup(c0, evac):
                    box = {}

                    def s0():
                        box["w"] = ws.get(hh)
                        box["p"] = nextpb()
                        for kc in range(8):
                            P.mm(box["p"][:], box["w"][:, kc, c0:c0 + 128], actT[:, kc, tsl], start=kc == 0, stop=False)

                    def s1():
                        for kc in range(8, 16):
                            P.mm(box["p"][:], box["w"][:, kc, c0:c0 + 128], actT[:, kc, tsl], start=False, stop=kc == 15)
                        evac(box["p"])
                    steps.append(s0)
                    steps.append(s1)

                if gla:
                    fgroup(0, lambda p: P.act(Q[:], p[:], AF.Copy, scale=128 ** -0.5))
                    fgroup(128, lambda p: P.copy(K_[:], p[:], eng="act"))

                    def gk():
                        p = nextpb()
                        P.mm(p[:], wgk[0:16, h * 128:(h + 1) * 128], glow[:, tsl], start=True, stop=True)
                        P.act(ptmp[:], p[:], AF.Exp, bias=negb[:, h:h + 1], scale=-1.0)
                        P.act(ptmp[:], ptmp[:], AF.Ln, bias=1.0)
                        P.ts(L[:], ptmp[:], -1.0 / 16.0, None, ALU.mult)
                    steps.append(gk)
                    def silu_ev(p, dst):
                        P.act(ptmp2[:], p[:], AF.Exp, scale=-1.0)
                        P.act(ptmp2[:], ptmp2[:], AF.Ln, bias=1.0)
                        P.act(ptmp2[:], ptmp2[:], AF.Exp, scale=-1.0)
                        P.tt(dst[:], ptmp2[:], p[:], ALU.mult)
                    for vc in range(2):
                        fgroup(512 + vc * 128, lambda p, vc=vc: silu_ev(p, G[vc]))
                    for hf in range(2):
                        def vhalf(hf=hf):
                            w = ws.get(hh)
                            p = nextpb()
                            for jj in range(2):
                                j = hf * 2 + jj
                                for kc in range(16):
                                    P.mm(p[:, jj * 256:(jj + 1) * 256], actT[:, kc, tt * 512 + j * 128: tt * 512 + (j + 1) * 128],
                                         w[:, kc, 256:512], start=kc == 0, stop=kc == 15)
                            P.copy(vt[:, hf * 2:hf * 2 + 2, :], p[:].rearrange("p (j v) -> p j v", v=256), eng="dve")
                        steps.append(vhalf)
                else:
                    def silu_ev(p, dst):
                        P.act(ptmp2[:], p[:], AF.Exp, scale=-1.0)
                        P.act(ptmp2[:], ptmp2[:], AF.Ln, bias=1.0)
                        P.act(ptmp2[:], ptmp2[:], AF.Exp, scale=-1.0)
                        P.tt(dst[:], ptmp2[:], p[:], ALU.mult)
                    fgroup(0, lambda p: silu_ev(p, Q))

                    def fev(p):
                        P.act(ptmp[:], p[:], AF.Exp, scale=-1.0)
                        P.act(L[:], ptmp[:], AF.Ln, bias=1.0, scale=lb[:, h:h + 1])
                        P.act(ptmp[:], ptmp[:], AF.Ln, bias=1.0)
                        P.tt(L[:], L[:], ptmp[:], ALU.subtract)
                        P.stt(ptmp[:], p[:], -1.0, ptmp[:], ALU.mult, ALU.subtract)
                        P.act(K_[:], ptmp[:], AF.Exp, bias=lnoml[:, h:h + 1])
                    fgroup(128, fev)

                    def sig_ev(p):
                        P.act(ptmp2[:], p[:], AF.Exp, scale=-1.0)
                        P.act(ptmp2[:], ptmp2[:], AF.Ln, bias=1.0)
                        P.act(G[0][:], ptmp2[:], AF.Exp, scale=-1.0)
                    fgroup(384, sig_ev)
                    for hf in range(2):
                        def vhalf(hf=hf):
                            w = ws.get(hh)
                            if hf == 0:
                                st["pv"] = nextpb()
                            p = st["pv"]
                            for jj in range(2):
                                j = hf * 2 + jj
                                for kc in range(16):
                                    P.mm(p[:, j * 128:(j + 1) * 128], actT[:, kc, tt * 512 + j * 128: tt * 512 + (j + 1) * 128],
                                         w[:, kc, 256:384], start=kc == 0, stop=kc == 15)
                            if hf == 1:
                                P.copy(vt[:, :, 0:128], p[:].rearrange("p (j v) -> p j v", v=128), eng="dve")
                        steps.append(vhalf)
                return steps

            def E_steps(u):
                hh, tt, gla, h, V = info(u)
                Q, K_, L = Qb[u % 2], Kb[u % 2], LG[u % 2]
                qd, kdt, dec, scb = QD[u % 2], KDT[u % 2], DEC[u % 2], SCB[u % 2]
                bv = bb[:].rearrange("p (c t) -> p c t", t=64)

                def e0():
                    P.scan(bb[:], rmask[:], L[:], 0.0, ALU.mult, ALU.add)
                    P.tt(d1[:].rearrange("p (c t) -> p c t", t=64), bv, bv[:, :, 31:32].to_broadcast([128, 8, 64]), ALU.subtract)
                    P.tt(d2[:].rearrange("p (c t) -> p c t", t=64), bv, bv[:, :, 63:64].to_broadcast([128, 8, 64]), ALU.subtract)
                    P.act(dec[:].rearrange("p (c o) -> p c o", o=1), bv[:, :, 63:64], AF.Exp)

                def e1():
                    P.act(exA[:], d1[:], AF.Exp)
                    P.tt(qs[:], Q[:], exA[:], ALU.mult)
                    P.act(exB[:], d1[:], AF.Exp, scale=-1.0)
                    P.tt(ks[:], K_[:], exB[:], ALU.mult)

                def e2():
                    P.act(exA[:], bb[:], AF.Exp)
                    P.tt(qd[:], Q[:], exA[:], ALU.mult)
                    P.act(exB[:], d2[:], AF.Exp, scale=-1.0)
                    P.tt(kd[:], K_[:], exB[:], ALU.mult)

                def e3():
                    for j in range(4):
                        P.transpose(ptr[:, j, :], kd[:, j * 128:(j + 1) * 128], ident[:])
                    P.copy(kdt[:], ptr[:], eng="act")

                def e4():
                    for j in range(4):
                        P.mm(psc[:, j * 128:(j + 1) * 128], ks[:, j * 128:(j + 1) * 128], qs[:, j * 128:(j + 1) * 128],
                             start=True, stop=True)
                    P.stt(scb[:], psc[:].rearrange("p (j t) -> p j t", t=128), 1e30,
                          pmask[:].rearrange("p (o t) -> p o t", o=1).to_broadcast([128, 4, 128]), ALU.min, ALU.mult)
                return [e0, e1, e2, e3, e4]

            def R_steps(u):
                hh, tt, gla, h, V = info(u)
                NV = V // 128
                qd, kdt, dec, scb, vt = QD[u % 2], KDT[u % 2], DEC[u % 2], SCB[u % 2], VT[u % 3]
                steps = []
                for c in range(8):
                    def step(c=c):
                        j, half = divmod(c, 2)
                        if c == 0 and tt == 0:
                            P.memset(S[:], 0.0)
                            st["cur"] = 0
                            P.memset(Sb[0][:], 0.0)
                        cur = st["cur"]
                        if half == 0:
                            for vc in range(NV):
                                P.mm(po[vc][:, j * 128:(j + 1) * 128], vt[:, j, vc * 128:(vc + 1) * 128], scb[:, j, :],
                                     start=True, stop=False, skip_group_check=True)
                        for vc in range(NV):
                            P.mm(po[vc][:, c * 64:(c + 1) * 64], Sb[cur][:, vc * 128:(vc + 1) * 128], qd[:, c * 64:(c + 1) * 64],
                                 start=False, stop=(half == 1), skip_group_check=True)
                        P.mm(pds[:, 0:V], kdt[half * 64:(half + 1) * 64, j, :], vt[half * 64:(half + 1) * 64, j, 0:V],
                             start=True, stop=True)
                        if dbg.get("sbcopy", "dve2") == "dve2":
                            P.stt(Sb[1 - cur][:, 0:V], S[:, 0:V], dec[:, c:c + 1], pds[:, 0:V], ALU.mult, ALU.add)
                            P.stt(S[:, 0:V], S[:, 0:V], dec[:, c:c + 1], pds[:, 0:V], ALU.mult, ALU.add)
                        else:
                            P.stt(S[:, 0:V], S[:, 0:V], dec[:, c:c + 1], pds[:, 0:V], ALU.mult, ALU.add)
                            P.copy(Sb[1 - cur][:, 0:V], S[:, 0:V], eng=dbg["sbcopy"])
                        st["cur"] = 1 - cur
                    steps.append(step)
                return steps

            def F_stage(u):
                hh, tt, gla, h, V = info(u)
                NV = V // 128
                G = GA[u % 4]
                for vc in range(NV):
                    P.copy(osb[vc][:], po[vc][:], eng="dve")
                    P.act(sqs[vc][:], po[vc][:], AF.Square)
                p = nextpb()
                for vc in range(NV):
                    P.mm(p[:], ones_f[:], sqs[vc][:], start=vc == 0, stop=vc == NV - 1)
                P.act(rstd[:], p[:], AF.Ln, bias=1e-6, scale=1.0 / V)
                P.act(rstd[:], rstd[:], AF.Exp, scale=-0.5)
                for vc in range(NV):
                    P.tt(osb[vc][:], osb[vc][:], rstd[:], ALU.mult)
                    gn = ggain[:, vc:vc + 1] if gla else hgain[:, 0:1]
                    sg = stg[st["stg"] % 4]
                    P.stt(sg[:], osb[vc][:], gn, G[vc][:], ALU.mult, ALU.mult)
                    r0 = (1024 + h * 256 + vc * 128) if gla else h * 128
                    P.dma("sp", actA[r0:r0 + 128, tt * 512:(tt + 1) * 512], sg[:], chan=f"st{st['stg'] % 4}")
                    st["stg"] += 1

            for r in range(-2, NU):
                A = P_steps(r + 2) if 0 <= r + 2 < NU else []
                B = E_steps(r + 1) if 0 <= r + 1 < NU else []
                C = R_steps(r) if 0 <= r < NU else []
                for k in range(max(len(A), len(B), len(C), 1)):
                    if k == 0:
                        for a in A[0:2]:
                            a()
                        if 0 <= r - 1 < NU:
                            F_stage(r - 1)
                    elif k >= 2 and k < len(A):
                        A[k]()
                    if k < len(C):
                        C[k]()
                    if k < len(B):
                        B[k]()
            F_stage(NU - 1)

    def xa_phase(l):
        W2d = xa_w_q[l]
        with Phase(P, f"xa{l}") as ph:
            actT = ph.sb("actT", [128, 16, T], BF16)
            wb = [ph.sb(f"wb{k}", [128, 16, 512], BF16) for k in range(2)]
            ws = WStream(wb, [(W2d, [(0, h_ * 512, 512)], 16) for h_ in range(4)])
            ws._issue(1)
            norm_T(f"xan{l}", lambda i: xs[i * 128:(i + 1) * 128, :], norm_xattn[l], actT, 16)
            kT = ph.sb("kT", [128, 16, MEM], BF16)
            vv = ph.sb("vv", [128, 2, D], BF16)
            qT = [ph.sb(f"qT{k}", [128, 4, T], BF16) for k in range(2)]
            eb_ = [ph.sb(f"e{k}", [128, 2, 512], BF16) for k in range(2)]
            lnd = [ph.sb(f"lnd{k}", [128, 512], F32) for k in range(2)]
            rinv = [ph.sb(f"rinv{k}", [128, 512], F32) for k in range(2)]
            stg = [ph.sb(f"stg{k}", [128, 512], BF16) for k in range(4)]
            pq = [ph.ps(f"pq{k}", [128, 512], F32) for k in range(2)]
            psS = [ph.ps(f"pss{k}", [128, 512], F32) for k in range(2)]
            pden = [ph.ps(f"pden{k}", [128, 512], F32) for k in range(2)]
            pov = [ph.ps(f"pov{k}", [128, 512], F32) for k in range(2)]
            P.dma("sp", kT[:], xaK[l].rearrange("(c p) m -> p c m", p=128), chan="kT")
            P.dma("sp", vv[:], xaV[l].rearrange("(b p) d -> p b d", p=128), chan="vv")
            cnt = dict(pq=0, ov=0)

            def projgroup(h, c, tt):
                w = ws.get(h)
                p = pq[cnt["pq"] % 2]
                cnt["pq"] += 1
                for kc in range(16):
                    P.mm(p[:], w[:, kc, c * 128:(c + 1) * 128], actT[:, kc, tt * 512:(tt + 1) * 512],
                         start=kc == 0, stop=kc == 15)
                P.act(qT[h % 2][:, c, tt * 512:(tt + 1) * 512], p[:], AF.Copy, scale=512 ** -0.5)

            def S1(u):
                h, tt = divmod(u, 4)
                q, e = qT[h % 2], eb_[u % 2]
                for mb in range(2):
                    p = psS[mb]
                    for c in range(4):
                        P.mm(p[:], kT[:, h * 4 + c, mb * 128:(mb + 1) * 128], q[:, c, tt * 512:(tt + 1) * 512],
                             start=c == 0, stop=c == 3)
                    P.act(e[:, mb, :], p[:], AF.Exp)

            def S2(u):
                e, pd = eb_[u % 2], pden[u % 2]
                for mb in range(2):
                    P.mm(pd[:], ones_b[:], e[:, mb, :], start=mb == 0, stop=mb == 1)
                P.act(lnd[u % 2][:], pd[:], AF.Ln)
                P.act(rinv[u % 2][:], lnd[u % 2][:], AF.Exp, scale=-1.0)

            def S3(u):
                h, tt = divmod(u, 4)
                e, ri = eb_[u % 2], rinv[u % 2]
                for c in range(4):
                    po_ = pov[cnt["ov"] % 2]
                    s = stg[cnt["ov"] % 4]
                    for mb in range(2):
                        P.mm(po_[:], vv[:, mb, h * 512 + c * 128: h * 512 + (c + 1) * 128], e[:, mb, :],
                             start=mb == 0, stop=mb == 1)
                    P.tt(s[:], po_[:], ri[:], ALU.mult)
                    r0 = (h * 4 + c) * 128
                    P.dma("sp", actA[r0:r0 + 128, tt * 512:(tt + 1) * 512], s[:], chan=f"st{cnt['ov'] % 4}")
                    cnt["ov"] += 1

            for c in range(4):
                for tt in range(4):
                    projgroup(0, c, tt)
            S1(0)
            for u in range(16):
                h, tt = divmod(u, 4)
                if h + 1 < 4:
                    for c in range(4):
                        projgroup(h + 1, c, tt)
                if u + 1 < 16:
                    S1(u + 1)
                S2(u)
                S3(u)

    def ffn_phase(l):
        W2d = ffn_w_in[l]
        with Phase(P, f"ff{l}") as ph:
            actT = ph.sb("actT", [128, 16, T], BF16)
            wb = [ph.sb(f"wb{k}", [128, 16, 512], BF16) for k in range(2)]
            ws = WStream(wb, [(W2d, [(0, wt_ * 256, 256), (256, DFF + wt_ * 256, 256)], 16) for wt_ in range(22)])
            ws._issue(1)
            norm_T(f"ffn{l}", lambda i: xs[i * 128:(i + 1) * 128, :], norm_ffn[l], actT, 16)
            gsb_ = [ph.sb(f"g{k}", [128, T + 2], F32) for k in range(2)]
            c1 = ph.sb("c1", [128, T], F32)
            c2 = ph.sb("c2", [128, T], F32)
            ab = [ph.sb(f"ab{k}", [128, T], BF16) for k in range(2)]
            PG = ph.ps("PG", [128, T], F32)
            PU = ph.ps("PU", [128, T], F32)
            for k in range(2):
                P.memset(gsb_[k][:, 0:2], 0.0)
            for wt in range(22):
                w = ws.get(wt)
                for cc in range(2):
                    j = wt * 2 + cc
                    g = gsb_[j % 2]
                    a = ab[j % 2]
                    for kc in range(16):
                        for tt in range(4):
                            P.mm(PG[:, tt * 512:(tt + 1) * 512], w[:, kc, 256 + cc * 128: 256 + (cc + 1) * 128],
                                 actT[:, kc, tt * 512:(tt + 1) * 512], start=kc == 0, stop=kc == 15)
                    P.copy(g[:, 2:T + 2], PG[:], eng="act")
                    for kc in range(16):
                        for tt in range(4):
                            P.mm(PU[:, tt * 512:(tt + 1) * 512], w[:, kc, cc * 128:(cc + 1) * 128],
                                 actT[:, kc, tt * 512:(tt + 1) * 512], start=kc == 0, stop=kc == 15)
                    P.ts(c1[:], g[:, 2:T + 2], cw[:, l, 2, j:j + 1], cb[:, l, j:j + 1], ALU.mult, ALU.add)
                    P.stt(c2[:], g[:, 1:T + 1], cw[:, l, 1, j:j + 1], c1[:], ALU.mult, ALU.add)
                    P.stt(c1[:], g[:, 0:T], cw[:, l, 0, j:j + 1], c2[:], ALU.mult, ALU.add)
                    P.act(c2[:], c1[:], AF.Silu)
                    P.tt(a[:], c2[:], PU[:], ALU.mult)
                    P.dma("sp", ffa[j * 128:(j + 1) * 128, :], a[:], chan=f"st{j % 2}")

    def sbqkv_phase():
        W2d = sb_w_qkv[0]
        with Phase(P, "sbp") as ph:
            actT = ph.sb("actT", [128, 16, T], BF16)
            wb = [ph.sb(f"wb{k}", [128, 16, 512], BF16) for k in range(2)]
            ws = WStream(wb, [(W2d, [(0, which_ * D + ft_ * 512, 512)], 16) for which_ in range(3) for ft_ in range(4)])
            ws._issue(1)
            norm_T("sbn", lambda i: xs[i * 128:(i + 1) * 128, :], norm_mix[1], actT, 16)
            stg = [ph.sb(f"stg{k}", [128, T], BF16) for k in range(2)]
            stv = [ph.sb(f"stv{k}", [128, 512], BF16) for k in range(4)]
            pq = [ph.ps(f"pq{k}", [128, T], F32) for k in range(2)]
            wi = 0
            u = 0
            for which in range(2):
                dst = sbq if which == 0 else sbk
                for ft in range(4):
                    w = ws.get(wi)
                    wi += 1
                    for c in range(4):
                        p = pq[u % 2]
                        s = stg[u % 2]
                        u += 1
                        for kc in range(16):
                            for tt in range(4):
                                P.mm(p[:, tt * 512:(tt + 1) * 512], w[:, kc, c * 128:(c + 1) * 128],
                                     actT[:, kc, tt * 512:(tt + 1) * 512], start=kc == 0, stop=kc == 15)
                        if which == 0:
                            P.act(s[:], p[:], AF.Copy, scale=128 ** -0.5)
                        else:
                            P.copy(s[:], p[:], eng="dve")
                        r0 = (ft * 4 + c) * 128
                        P.dma("sp", dst[r0:r0 + 128, :], s[:], chan=f"st{(u - 1) % 2}")
            v = 0
            for ft in range(4):
                w = ws.get(wi)
                wi += 1
                for i in range(16):
                    p = pq[v % 2]
                    s = stv[v % 4]
                    v += 1
                    for kc in range(16):
                        P.mm(p[:, 0:512], actT[:, kc, i * 128:(i + 1) * 128], w[:, kc, :], start=kc == 0, stop=kc == 15)
                    P.copy(s[:], p[:, 0:512], eng="act" if i % 2 else "dve")
                    P.dma("sp", sbv[i * 128:(i + 1) * 128, ft * 512:(ft + 1) * 512], s[:], chan=f"sv{(v - 1) % 4}")

    def sbatt_phase():
        with Phase(P, "sba") as ph:
            qh = [ph.sb(f"qh{k}", [128, T], BF16) for k in range(2)]
            kh = [ph.sb(f"kh{k}", [128, T], BF16) for k in range(2)]
            vh = [ph.sb(f"vh{k}", [128, 16, 128], BF16) for k in range(2)]
            NBUF = 3
            ez = [ph.sb(f"ez{k}", [128, 512], F32) for k in range(NBUF)]
            sp = [ph.sb(f"sp{k}", [128, 512], F32) for k in range(NBUF)]
            cum = [ph.sb(f"cum{k}", [128, 512], F32) for k in range(2)]
            att = [ph.sb(f"att{k}", [128, 512], BF16) for k in range(NBUF)]
            stg = [ph.sb(f"stg{k}", [128, T], BF16) for k in range(2)]
            pz = [ph.ps(f"pz{k}", [128, 512], F32) for k in range(3)]
            pl = pz
            po = [ph.ps(f"po{k}", [128, 512], F32) for k in range(2)]
            sbm_f = ph.sb("sbm_f", [128, 4, 512], F32)
            sbm_b = ph.sb("sbm_b", [128, 4, 512], BF16)
            for j in range(4):
                P.memset(sbm_f[:, j, :], 1.0, eng="pool")
                P.aselect(sbm_f[:, j, :], sbm_f[:, j, :], ALU.is_gt, 0.0, -128 * j, [[1, 512]], -1)
            P.copy(sbm_b[:], sbm_f[:], eng="pool")

            def load_head(h):
                P.dma("sp", qh[h % 2][:], sbq[h * 128:(h + 1) * 128, :], chan=f"ldq{h % 2}")
                P.dma("sp", kh[h % 2][:], sbk[h * 128:(h + 1) * 128, :], chan=f"ldk{h % 2}")
                P.dma("sp", vh[h % 2][:], sbv[:, h * 128:(h + 1) * 128].rearrange("(b p) d -> p b d", p=128),
                      chan=f"ldv{h % 2}", allow_slow_non_contiguous=True)

            its = []
            g = 0
            for h in range(16):
                for TT in range(4):
                    nb = 4 * TT + 4
                    for bi, B in enumerate(range(nb - 1, -1, -1)):
                        its.append(dict(h=h, TT=TT, B=B, bi=bi, g=g, last=(B == 0)))
                    g += 1
            n = len(its)

            def ctx(i):
                it = its[i]
                h, TT, B = it["h"], it["TT"], it["B"]
                q, k, v = qh[h % 2], kh[h % 2], vh[h % 2]
                return (it, q[:, TT * 512:(TT + 1) * 512], k[:, B * 128:(B + 1) * 128], v[:, B, :],
                        pz[i % 3], pl[i % 3], ez[i % NBUF], sp[i % NBUF], att[i % NBUF],
                        cum[it["g"] % 2], po[it["g"] % 2], B - 4 * TT)

            def stA(i):
                it, qt, kb, vb, z, lp, e, s_, a, cm, o, jd = ctx(i)
                if it["TT"] == 0 and it["bi"] == 0:
                    if it["h"] == 0:
                        load_head(0)
                    if it["h"] + 1 < 16:
                        load_head(it["h"] + 1)
                c0 = 128 * max(jd, 0)
                P.mm(z[:, c0:], kb, qt[:, c0:], start=True, stop=False, skip_group_check=True)
                P.act(e[:, c0:], z[:, c0:], AF.Exp)
                P.act(s_[:, c0:], e[:, c0:], AF.Ln, bias=1.0)
                if jd >= 0:
                    P.tt(s_[:, c0:c0 + 128], s_[:, c0:c0 + 128], sbm_f[:, 0, 0:128], ALU.mult, eng="pool")

            def stC(i):
                it, qt, kb, vb, z, lp, e, s_, a, cm, o, jd = ctx(i)
                bi, B = it["bi"], it["B"]
                lp = z
                c0 = 128 * max(jd, 0)
                P.mm(lp[:, c0:], negtri[:], s_[:, c0:], start=False, stop=(bi == 0), skip_group_check=True)
                if bi > 0:
                    P.mm(lp[:, c0:], negones[:], cm[:, c0:], start=False, stop=True, skip_group_check=True)
                if bi == 0:
                    if c0 > 0:
                        P.memset(cm[:, 0:c0], 0.0)
                    P.copy(cm[:, c0:], s_[:, c0:], eng="dve")
                elif B > 0:
                    P.tt(cm[:, c0:], cm[:, c0:], s_[:, c0:], ALU.add)
                P.act(a[:, c0:], lp[:, c0:], AF.Exp)
                if jd >= 0:
                    P.tt(a[:, c0:c0 + 128], a[:, c0:c0 + 128], sbm_b[:, 0, 0:128], ALU.mult, eng="pool")

            def stF(i):
                it, qt, kb, vb, z, lp, e, s_, a, cm, o, jd = ctx(i)
                h, TT = it["h"], it["TT"]
                c0 = 128 * max(jd, 0)
                P.mm(o[:, c0:], vb, a[:, c0:], start=(it["bi"] == 0), stop=it["last"], skip_group_check=True)
                if it["last"]:
                    s = stg[h % 2]
                    P.copy(s[:, TT * 512:(TT + 1) * 512], o[:], eng="dve")
                    if TT == 3:
                        P.dma("sp", actA[h * 128:(h + 1) * 128, :], s[:], chan=f"st{h % 2}")

            for r in range(n + 2):
                if r < n:
                    stA(r)
                if 0 <= r - 1 < n:
                    stC(r - 1)
                if 0 <= r - 2 < n:
                    stF(r - 2)

    def final_phase():
        with Phase(P, "fin") as ph:
            gbc = ph.sb("gbc", [128, D], F32)
            P.dma("sp", gbc[:], final_norm.partition_broadcast(128), chan="gbc")
            xt = [ph.sb(f"xt{k}", [128, D], F32) for k in range(3)]
            junk = ph.sb("junk", [128, D], BF16)
            ss = [ph.sb(f"ss{k}", [128, 1], F32) for k in range(3)]
            for i in range(16):
                b = i % 3
                P.dma("sp", xt[b][:], xs[i * 128:(i + 1) * 128, :], chan=f"fx{b}")
                P.act(junk[:], xt[b][:], AF.Square, accum_out=ss[b][:])
                P.act(ss[b][:], ss[b][:], AF.Sqrt, bias=1e-6, scale=1.0 / D)
                P.recip(ss[b][:], ss[b][:])
                P.stt(xt[b][:], xt[b][:], ss[b][:], gbc[:], ALU.mult, ALU.mult)
                P.dma("pool", out[i * 128:(i + 1) * 128, :], xt[b][:], chan=f"fx{b}")

    def run():
        mem_phase()
        if stopped("mem"):
            return
        ab_phase(x_in)
        if stopped("ab"):
            return
        outproj("abo", actA, D, ab_w_out[0], x_in, xs)
        if stopped("abo"):
            return
        for l in range(2):
            if l == 1:
                sbqkv_phase()
                if stopped("sbp"):
                    return
                sbatt_phase()
                if stopped("sba"):
                    return
                outproj("sbo", actA, D, sb_w_out[0], xs, xs)
                if stopped("sbo"):
                    return
            xa_phase(l)
            if stopped(f"xa{l}"):
                return
            outproj(f"xao{l}", actA, D, xa_w_o[l], xs, xs)
            if stopped(f"xao{l}"):
                return
            ffn_phase(l)
            if stopped(f"ff{l}"):
                return
            outproj(f"ffo{l}", ffa, DFF, ffn_w_out[l], xs, xs)
            if stopped(f"ffo{l}"):
                return
        final_phase()

    run()
    P.finish(top)
    top.close()
    return nc, P


INPUT_NAMES = ["mem_norm", "norm_mix", "norm_xattn", "norm_ffn", "ab_w_in", "hgrn_lb_logits", "hgrn_norm",
               "gla_w_gk", "gla_b_gk", "gla_norm", "ab_w_out", "sb_w_qkv", "sb_w_out", "xa_w_q", "xa_w_kv",
               "xa_w_o", "ffn_w_in", "ffn_conv_w", "ffn_conv_b", "ffn_w_out", "final_norm"]


def kernel(**inputs):
    n = 8
    nc, _ = build()
    shared = {k: np.ascontiguousarray(np.asarray(inputs[k], dtype=np.float32)) for k in INPUT_NAMES}
    x = np.asarray(inputs["x"], dtype=np.float32)
    mem = np.asarray(inputs["mem"], dtype=np.float32)
    in_maps = []
    for b in range(n):
        m = dict(shared)
        m["x"] = np.ascontiguousarray(x[b])
        m["mem"] = np.ascontiguousarray(mem[b])
        in_maps.append(m)
    res = run_bass_kernel_spmd(nc, in_maps, core_ids=list(range(n)))
    return np.stack([np.asarray(r["out"], dtype=np.float32) for r in res.results], axis=0)
```
